# Optimizing a Trainium2 kernel written in Bass

```python
import math
import jax, jax.numpy as jnp
from jax import lax
import numpy as np

D_MODEL = 1024
BATCH = 4
SEQ = 8192
DEPTH = 2

D_MIX = D_MODEL
ATT_HEADS = 8
ATT_HD = 64
D_ATT = ATT_HEADS * ATT_HD
D_CONV = 256
CONV_GROUPS = 4
CONV_K = 31
D_RNN = 256
RNN_BLOCKS = 4
RNN_BD = D_RNN // RNN_BLOCKS
RNN_CONV_K = 4
RG_C = 8.0
Q_BLOCK = 128
EPS = 1e-6

Q0, Q1 = 0, D_ATT
K0, K1 = Q1, Q1 + D_ATT
V0, V1 = K1, K1 + D_ATT
F0, F1 = V1, V1 + ATT_HEADS
C0, C1 = F1, F1 + 2 * D_CONV
R0, R1 = C1, C1 + D_RNN
G0, G1 = R1, R1 + D_RNN
N_IN = G1

PEER_HEADS = 8
N_KEYS = 128
N_EXPERTS = N_KEYS * N_KEYS
D_KEY = 256
D_HALF = D_KEY // 2
TOPK = 16
TOK_CHUNK = 128

kernel_name = "hymba_fox_conformer_rglru_peer"


def rms_norm(x, g):
    xf = x.astype(jnp.float32)
    y = xf * lax.rsqrt(jnp.mean(xf * xf, axis=-1, keepdims=True) + EPS)
    return (y * g.astype(jnp.float32)).astype(x.dtype)


def layer_norm(x, g, b):
    xf = x.astype(jnp.float32)
    mu = jnp.mean(xf, axis=-1, keepdims=True)
    var = jnp.mean(jnp.square(xf - mu), axis=-1, keepdims=True)
    y = (xf - mu) * lax.rsqrt(var + EPS)
    return (y * g.astype(jnp.float32) + b.astype(jnp.float32)).astype(x.dtype)


def causal_dwconv(x, w, b):
    k = w.shape[0]
    y = lax.conv_general_dilated(
        x, w[:, None, :], window_strides=(1,), padding=[(k - 1, 0)],
        dimension_numbers=("NWC", "WIO", "NWC"), feature_group_count=x.shape[-1])
    return y + b


def forgetting_attention(q, k, v, log_f):
    b, s, h, hd = q.shape
    nb = s // Q_BLOCK
    scale = 1.0 / math.sqrt(hd)
    cum = jnp.cumsum(log_f.astype(jnp.float32), axis=1).transpose(0, 2, 1)
    q = q.transpose(0, 2, 1, 3)
    k = k.transpose(0, 2, 1, 3)
    v = v.transpose(0, 2, 1, 3)
    qb = q.reshape(b, h, nb, Q_BLOCK, hd).transpose(2, 0, 1, 3, 4)
    cb = cum.reshape(b, h, nb, Q_BLOCK).transpose(2, 0, 1, 3)
    kpos = jnp.arange(s)

    def one_block(args):
        qi, ci, blk = args
        qpos = blk * Q_BLOCK + jnp.arange(Q_BLOCK)
        logits = jnp.einsum("bhqd,bhkd->bhqk", qi, k).astype(jnp.float32) * scale
        logits = logits + ci[..., :, None] - cum[:, :, None, :]
        logits = jnp.where(kpos[None, :] <= qpos[:, None], logits, -jnp.inf)
        p = jax.nn.softmax(logits, axis=-1)
        return jnp.einsum("bhqk,bhkd->bhqd", p.astype(v.dtype), v)

    o = lax.map(one_block, (qb, cb, jnp.arange(nb)))
    return o.transpose(1, 0, 3, 2, 4).reshape(b, s, h * hd)


def conformer_conv(u, w_dw, b_dw, ln_g, ln_b):
    a, gate = u[..., :D_CONV], u[..., D_CONV:]
    y = a * jax.nn.sigmoid(gate)
    y = causal_dwconv(y, w_dw, b_dw)
    y = layer_norm(y, ln_g, ln_b)
    return jax.nn.silu(y)


def _lin_combine(c1, c2):
    a1, b1 = c1
    a2, b2 = c2
    return a1 * a2, a2 * b1 + b2


def rglru_block(xr, gate_in, conv_w, conv_b, w_r, b_r, w_i, b_i, lam):
    xr = causal_dwconv(xr, conv_w, conv_b)
    b, s, c = xr.shape
    xh = xr.reshape(b, s, RNN_BLOCKS, RNN_BD)
    r = jax.nn.sigmoid(jnp.einsum("bshi,hij->bshj", xh, w_r).reshape(b, s, c) + b_r)
    i = jax.nn.sigmoid(jnp.einsum("bshi,hij->bshj", xh, w_i).reshape(b, s, c) + b_i)
    log_a = -RG_C * r.astype(jnp.float32) * jax.nn.softplus(-lam.astype(jnp.float32))
    a = jnp.exp(log_a)
    mult = jnp.sqrt(-jnp.expm1(2.0 * log_a))
    bterm = mult * (i * xr).astype(jnp.float32)
    _, hs = lax.associative_scan(_lin_combine, (a, bterm), axis=1)
    return hs.astype(xr.dtype) * jax.nn.gelu(gate_in)


def peer(xn, wq, k1, k2, u_tab, v_tab):
    b, s, d = xn.shape
    t = b * s
    xt = xn.reshape(t, d)
    q = (xt @ wq).reshape(t, PEER_HEADS, D_KEY)
    q1, q2 = q[..., :D_HALF], q[..., D_HALF:]
    s1 = jnp.einsum("thd,hnd->thn", q1, k1).astype(jnp.float32)
    s2 = jnp.einsum("thd,hnd->thn", q2, k2).astype(jnp.float32)
    v1, i1 = lax.top_k(s1, TOPK)
    v2, i2 = lax.top_k(s2, TOPK)
    cand = (v1[..., :, None] + v2[..., None, :]).reshape(t, PEER_HEADS, TOPK * TOPK)
    sv, ci = lax.top_k(cand, TOPK)
    e1 = jnp.take_along_axis(i1, ci // TOPK, axis=-1)
    e2 = jnp.take_along_axis(i2, ci % TOPK, axis=-1)
    nc = t // TOK_CHUNK
    experts = (e1 * N_KEYS + e2).reshape(nc, TOK_CHUNK, PEER_HEADS * TOPK)
    gates = jax.nn.softmax(sv, axis=-1).astype(xn.dtype).reshape(nc, TOK_CHUNK, PEER_HEADS * TOPK)
    xc = xt.reshape(nc, TOK_CHUNK, d)

    def chunk(args):
        xi, ei, gi = args
        hid = jax.nn.gelu(jnp.einsum("td,tkd->tk", xi, u_tab[ei]))
        return jnp.einsum("tk,tkd->td", gi * hid, v_tab[ei])

    out = lax.map(chunk, (xc, experts, gates))
    return out.reshape(b, s, d)


def setup_inputs(seed: int = 0) -> dict:
    key = jax.random.key(seed)
    ks = jax.random.split(key, 26)
    L, D = DEPTH, D_MODEL
    nrm = jax.random.normal
    u_lr = jax.random.uniform(ks[14], (L, D_RNN), minval=0.9, maxval=0.999)
    p_lr = u_lr ** (1.0 / RG_C)
    lam = jnp.log(p_lr) - jnp.log1p(-p_lr)
    return {
        "x": nrm(ks[0], (BATCH, SEQ, D), jnp.float32),
        "norm1_g": 1.0 + 0.01 * nrm(ks[1], (L, D)),
        "w_in": nrm(ks[2], (L, D, N_IN)) * D ** -0.5,
        "b_forget": jax.random.uniform(ks[3], (L, ATT_HEADS), minval=1.0, maxval=5.0),
        "conv_dw_w": nrm(ks[4], (L, CONV_K, D_CONV)) * CONV_K ** -0.5,
        "conv_dw_b": 0.01 * nrm(ks[5], (L, D_CONV)),
        "conv_ln_g": 1.0 + 0.01 * nrm(ks[6], (L, D_CONV)),
        "conv_ln_b": 0.01 * nrm(ks[7], (L, D_CONV)),
        "rg_conv_w": nrm(ks[8], (L, RNN_CONV_K, D_RNN)) * RNN_CONV_K ** -0.5,
        "rg_conv_b": 0.01 * nrm(ks[9], (L, D_RNN)),
        "rg_w_r": nrm(ks[10], (L, RNN_BLOCKS, RNN_BD, RNN_BD)) * RNN_BD ** -0.5,
        "rg_b_r": 0.01 * nrm(ks[11], (L, D_RNN)),
        "rg_w_i": nrm(ks[12], (L, RNN_BLOCKS, RNN_BD, RNN_BD)) * RNN_BD ** -0.5,
        "rg_b_i": 0.01 * nrm(ks[13], (L, D_RNN)),
        "rg_lambda": lam,
        "w_out": nrm(ks[15], (L, D_MIX, D)) * D_MIX ** -0.5,
        "norm2_g": 1.0 + 0.01 * nrm(ks[16], (L, D)),
        "peer_wq": nrm(ks[17], (L, D, PEER_HEADS * D_KEY)) * D ** -0.5,
        "peer_k1": nrm(ks[18], (L, PEER_HEADS, N_KEYS, D_HALF)) * D_HALF ** -0.5,
        "peer_k2": nrm(ks[19], (L, PEER_HEADS, N_KEYS, D_HALF)) * D_HALF ** -0.5,
        "peer_u": nrm(ks[20], (L, N_EXPERTS, D)) * D ** -0.5,
        "peer_v": nrm(ks[21], (L, N_EXPERTS, D)) * 0.5 * PEER_HEADS ** -0.5,
        "final_g": 1.0 + 0.01 * nrm(ks[22], (D,)),
    }


def reference(x, norm1_g, w_in, b_forget, conv_dw_w, conv_dw_b, conv_ln_g, conv_ln_b,
              rg_conv_w, rg_conv_b, rg_w_r, rg_b_r, rg_w_i, rg_b_i, rg_lambda, w_out,
              norm2_g, peer_wq, peer_k1, peer_k2, peer_u, peer_v, final_g):
    b, s, _ = x.shape
    for l in range(DEPTH):
        h = rms_norm(x, norm1_g[l])
        proj = h @ w_in[l]
        q = proj[..., Q0:Q1].reshape(b, s, ATT_HEADS, ATT_HD)
        k = proj[..., K0:K1].reshape(b, s, ATT_HEADS, ATT_HD)
        v = proj[..., V0:V1].reshape(b, s, ATT_HEADS, ATT_HD)
        log_f = jax.nn.log_sigmoid((proj[..., F0:F1] + b_forget[l]).astype(jnp.float32))
        y_att = forgetting_attention(q, k, v, log_f)
        y_conv = conformer_conv(proj[..., C0:C1], conv_dw_w[l], conv_dw_b[l],
                                conv_ln_g[l], conv_ln_b[l])
        y_rnn = rglru_block(proj[..., R0:R1], proj[..., G0:G1], rg_conv_w[l], rg_conv_b[l],
                            rg_w_r[l], rg_b_r[l], rg_w_i[l], rg_b_i[l], rg_lambda[l])
        mixed = jnp.concatenate([y_att, y_conv.astype(x.dtype), y_rnn.astype(x.dtype)], axis=-1)
        x = x + mixed @ w_out[l]
        h2 = rms_norm(x, norm2_g[l])
        x = x + peer(h2, peer_wq[l], peer_k1[l], peer_k2[l], peer_u[l], peer_v[l])
    return rms_norm(x, final_g)
```

```python
import contextlib
import numpy as np
import ml_dtypes
import concourse.bass as bass
import concourse.mybir as mybir
from concourse.bass_utils import run_bass_kernel_spmd

F32 = mybir.dt.float32
BF16 = mybir.dt.bfloat16
U32 = mybir.dt.uint32
I32 = mybir.dt.int32
U8 = mybir.dt.uint8
ALU = mybir.AluOpType
AF = mybir.ActivationFunctionType
AX = mybir.AxisListType
DSZ = {F32: 4, BF16: 2, U32: 4, I32: 4, U8: 1}

D = 1024
N_IN = 2568
Q0, K0, V0, FG0, C0, R0, G0 = 0, 512, 1024, 1536, 1544, 2056, 2312
EPS = 1e-6
SEM_LIMIT = 30000
MASKV = -30000.0


class _Cnt:
    def __init__(self, prog, name):
        self.prog, self.name = prog, name
        self.sem, self.val, self.last, self.n = None, 0, None, 0

    def bump(self, inc):
        if self.sem is None or self.val + inc > SEM_LIMIT:
            self.sem = self.prog._new_sem(f"{self.name}_{self.n}")
            self.n += 1
            self.val = 0
        self.val += inc
        self.last = (self.sem, self.val)
        return self.last


class Prog:
    ENG = ("pe", "act", "dve", "pool", "sp")

    def __init__(self, nc):
        self.nc = nc
        self.stack = contextlib.ExitStack()
        self.ops = {e: [] for e in self.ENG}
        self.cnt = {e: _Cnt(self, "e" + e) for e in self.ENG}
        self.chan = {}
        self.last_w = {}
        self.readers = {}
        self.seen = {e: {} for e in self.ENG}
        self.semown = {}
        self.out_tokens = []
        self.bar = []
        self.nops = 0

    def _new_sem(self, name):
        return self.stack.enter_context(self.nc.semaphore(name))

    def sb(self, name, shape, dtype):
        return self.stack.enter_context(self.nc.sbuf_tensor(name, list(shape), dtype))

    def ps(self, name, shape, dtype):
        return self.stack.enter_context(self.nc.psum_tensor(name, list(shape), dtype))

    def barrier(self):
        toks = [c.last for c in self.cnt.values() if c.last is not None]
        toks += [c.last for n, c in self.chan.items() if c.last is not None and not str(n).startswith("cv")]
        self.bar = toks

    def op(self, eng, fn, r=(), w=(), chan=None, final=False):
        self.nops += 1
        deps = list(self.bar)
        for k in r:
            t = self.last_w.get(k)
            if t is not None:
                deps.append(t)
        for k in w:
            t = self.last_w.get(k)
            if t is not None:
                deps.append(t)
            deps.extend(self.readers.get(k, ()))
        if chan is not None:
            c = self.chan.get(chan)
            if c is None:
                c = self.chan[chan] = _Cnt(self, "c" + str(chan))
            if c.last is not None:
                deps.append(c.last)
            tok = c.bump(16)
            inc = 16
        else:
            tok = self.cnt[eng].bump(1)
            self.semown[id(tok[0])] = eng
            inc = 1
        waits = {}
        for (s, v) in deps:
            if eng == "pe" and chan is None and self.semown.get(id(s)) == "pe":
                continue
            sid = id(s)
            if self.seen[eng].get(sid, 0) >= v:
                continue
            if sid not in waits or waits[sid][1] < v:
                waits[sid] = (s, v)
        for sid, (s, v) in waits.items():
            self.seen[eng][sid] = v
        self.ops[eng].append((list(waits.values()), fn, tok[0], inc))
        for k in r:
            if k not in w:
                self.readers.setdefault(k, []).append(tok)
        for k in w:
            self.last_w[k] = tok
            self.readers[k] = []
        if final:
            self.out_tokens.append(tok)
        return tok

    def dma(self, eng, out, in_, r=(), w=(), chan=None, final=False, **kw):
        return self.op(eng, lambda e: e.dma_start(out=out, in_=in_, **kw), r=r, w=w,
                       chan=chan, final=final)

    def emit(self):
        nc = self.nc
        fin = {}
        for (s, v) in self.out_tokens:
            if id(s) not in fin or fin[id(s)][1] < v:
                fin[id(s)] = (s, v)
        with nc.Block() as block:
            def run(engname):
                def body(e):
                    for (waits, fn, sem, inc) in self.ops[engname]:
                        for (s, v) in waits:
                            e.wait_ge(s, v)
                        fn(e).then_inc(sem, inc)
                    if engname == "sp":
                        for (s, v) in fin.values():
                            e.wait_ge(s, v)
                return body
            block.tensor(run("pe"))
            block.scalar(run("act"))
            block.vector(run("dve"))
            block.gpsimd(run("pool"))
            block.sync(run("sp"))
        self.stack.close()


class Arena:
    def __init__(self, P, nbytes):
        self.t = P.sb("arena", [128, nbytes], U8)
        self.nbytes = nbytes
        self.off = 0
        self.marks = []

    def mark(self):
        self.marks.append(self.off)

    def release(self):
        self.off = self.marks.pop()

    def alloc(self, free_shape, dtype):
        n = int(np.prod(free_shape)) * DSZ[dtype]
        n_al = (n + 63) // 64 * 64
        assert self.off + n_al <= self.nbytes, (self.off, n_al, self.nbytes)
        ap = self.t[:, self.off:self.off + n].bitcast(dtype)
        self.off += n_al
        if len(free_shape) == 2:
            ap = ap.rearrange("p (a b) -> p a b", b=free_shape[1])
        elif len(free_shape) == 3:
            ap = ap.rearrange("p (a b c) -> p a b c", b=free_shape[1], c=free_shape[2])
        return ap


def _dram(nc, name, shape, dtype, kind):
    return nc.dram_tensor(name, list(shape), dtype, kind=kind).ap()


class Ctx:
    pass


def setup_common(P, A, banks):
    C = Ctx()
    C.identf = A.alloc([128], F32)
    C.identb = A.alloc([128], BF16)
    C.onesf = A.alloc([128], F32)
    C.maskT = A.alloc([128], BF16)
    mf = A.alloc([128], F32)
    P.op("pool", lambda e: e.memset(C.identf, 0.0), w=["identf"])
    P.op("pool", lambda e: e.affine_select(out=C.identf, in_=C.identf, pattern=[[-1, 128]],
                                           compare_op=ALU.not_equal, fill=1.0, base=0,
                                           channel_multiplier=1), r=["identf"], w=["identf"])
    P.op("dve", lambda e: e.tensor_copy(out=C.identb, in_=C.identf), r=["identf"], w=["identb"])
    P.op("pool", lambda e: e.memset(C.onesf, 1.0), w=["onesf"])
    P.op("pool", lambda e: e.memset(mf, 0.0), w=["mf"])
    P.op("pool", lambda e: e.affine_select(out=mf, in_=mf, pattern=[[1, 128]],
                                           compare_op=ALU.is_ge, fill=MASKV, base=0,
                                           channel_multiplier=-1), r=["mf"], w=["mf"])
    P.op("dve", lambda e: e.tensor_copy(out=C.maskT, in_=mf), r=["mf"], w=["maskT"])
    return C


def phase1(P, A, banks, C, T, io):
    NB = T // 128
    NCH = T // 512
    A.mark()
    w_sb = A.alloc([8, N_IN], BF16)
    g_bc = A.alloc([D], F32)
    bfg = A.alloc([1], F32)
    nbfg = A.alloc([1], F32)
    xt = [A.alloc([D], F32) for _ in range(4)]
    sqj = A.alloc([D], F32)
    ss = [A.alloc([1], F32) for _ in range(2)]
    hb = [A.alloc([D], BF16) for _ in range(2)]
    hT = [A.alloc([8, 512], BF16) for _ in range(2)]
    stg = [A.alloc([512], F32) for _ in range(4)]
    stb = [A.alloc([512], BF16) for _ in range(4)]
    sg = [A.alloc([512], F32) for _ in range(2)]
    lf = A.alloc([T], F32)
    cum = A.alloc([T], F32)
    cq3 = A.alloc([3, T], BF16)
    kbs = A.alloc([NB, 8], F32)
    onecol = A.alloc([1], F32)

    wv = io["w_in"].rearrange("(kc p) n -> p kc n", p=128)
    for kc in range(8):
        for hf in range(2):
            cs = slice(1284 * hf, 1284 * (hf + 1))
            P.dma("pool", w_sb[:, kc, cs], wv[:, kc, cs], w=["w_sb"], chan=f"w1_{(2 * kc + hf) % 4}")
    P.dma("sp", g_bc, io["g1"].to_broadcast([128, D]), w=["g_bc"], chan="c0")
    P.dma("sp", bfg[0:8, :], io["bfg"], w=["bfg"], chan="c1")
    P.op("dve", lambda e: e.tensor_scalar(out=nbfg[0:8, :], in0=bfg[0:8, :], scalar1=-1.0, scalar2=None,
                                          op0=ALU.mult), r=["bfg"], w=["nbfg"])
    P.op("pool", lambda e: e.memset(onecol, 1.0), w=["onecol"])

    fm = []
    for j in range(4):
        fm.append((Q0 + 128 * j, 128, "q", 128 * j))
    for j in range(4):
        fm.append((K0 + 128 * j, 128, "k", 128 * j))
    for j in range(2):
        fm.append((C0 + 256 + 128 * j, 128, "cg", 128 * j))
        fm.append((C0 + 128 * j, 128, "ca", 128 * j))
    for j in range(2):
        fm.append((R0 + 128 * j, 128, "r", 128 * j))
    for j in range(2):
        fm.append((G0 + 128 * j, 128, "g", 128 * j))
    fm.append((FG0, 8, "f", 0))

    xv = io["x"].rearrange("(n p) d -> n p d", p=128)
    cnt = {"pz": 0, "stg": 0, "stb": 0, "sg": 0}
    for c in range(NCH):
        hs = c % 2
        for tt in range(4):
            i = 4 * c + tt
            xs = i % 4
            s2 = i % 2
            P.dma("sp", xt[xs], xv[i], w=[f"xt{xs}"], chan=f"x{xs}")
            P.op("act", lambda e, xs=xs, s2=s2: e.activation(out=sqj, in_=xt[xs], func=AF.Square,
                                                              accum_out=ss[s2]),
                 r=[f"xt{xs}"], w=["sqj", f"ss{s2}"])
            P.op("act", lambda e, s2=s2: e.activation(out=ss[s2], in_=ss[s2], func=AF.Sqrt,
                                                      scale=1.0 / D, bias=EPS),
                 r=[f"ss{s2}"], w=[f"ss{s2}"])
            P.op("dve", lambda e, s2=s2: e.reciprocal(out=ss[s2], in_=ss[s2]), r=[f"ss{s2}"], w=[f"ss{s2}"])
            P.op("dve", lambda e, xs=xs, s2=s2: e.scalar_tensor_tensor(
                out=hb[s2], in0=xt[xs], scalar=ss[s2], in1=g_bc, op0=ALU.mult, op1=ALU.mult),
                r=[f"xt{xs}", f"ss{s2}", "g_bc"], w=[f"hb{s2}"])
            tb = 6 + s2
            tpv = banks[tb].bitcast(BF16).rearrange("p (a b) -> p a b", b=128)
            for kc in range(8):
                P.op("pe", lambda e, kc=kc, s2=s2, tpv=tpv: e.transpose(
                    out=tpv[:, kc, :], in_=hb[s2][:, 128 * kc:128 * (kc + 1)], identity=C.identb),
                    r=[f"hb{s2}", "identb"], w=[f"bank{tb}"])
            P.op("act" if tt % 2 == 0 else "dve",
                 (lambda e, tpv=tpv, hs=hs, tt=tt: e.copy(out=hT[hs][:, :, 128 * tt:128 * (tt + 1)], in_=tpv))
                 if tt % 2 == 0 else
                 (lambda e, tpv=tpv, hs=hs, tt=tt: e.tensor_copy(out=hT[hs][:, :, 128 * tt:128 * (tt + 1)], in_=tpv)),
                 r=[f"bank{tb}"], w=[f"hT{hs}"])
        tsl = slice(512 * c, 512 * (c + 1))
        for (col0, wd, kind, row0) in fm:
            pb = cnt["pz"] % 4
            cnt["pz"] += 1
            pz = banks[pb]
            for kc in range(8):
                P.op("pe", lambda e, kc=kc, pz=pz, col0=col0, wd=wd, hs=hs: e.matmul(
                    out=pz[0:wd, :], lhsT=w_sb[:, kc, col0:col0 + wd], rhs=hT[hs][:, kc, :],
                    start=(kc == 0), stop=(kc == 7)),
                    r=["w_sb", f"hT{hs}"], w=[f"bank{pb}"])
            if kind in ("q", "k"):
                sb_i = cnt["stb"] % 4
                cnt["stb"] += 1
                eng = "act" if (sb_i % 2 == 0) else "dve"
                if eng == "act":
                    P.op("act", lambda e, pz=pz, sb_i=sb_i: e.copy(out=stb[sb_i], in_=pz),
                         r=[f"bank{pb}"], w=[f"stb{sb_i}"])
                else:
                    P.op("dve", lambda e, pz=pz, sb_i=sb_i: e.tensor_copy(out=stb[sb_i], in_=pz),
                         r=[f"bank{pb}"], w=[f"stb{sb_i}"])
                dst = io["qT" if kind == "q" else "kT"][row0:row0 + 128, tsl]
                P.dma("sp", dst, stb[sb_i], r=[f"stb{sb_i}"], w=["o_" + kind], chan=f"ob{sb_i}", final=True)
            elif kind == "cg":
                gi = cnt["sg"] % 2
                cnt["sg"] += 1
                P.op("act", lambda e, pz=pz, gi=gi: e.activation(out=sg[gi], in_=pz, func=AF.Sigmoid),
                     r=[f"bank{pb}"], w=[f"sg{gi}"])
                last_sg = gi
            elif kind == "ca":
                si = cnt["stg"] % 4
                cnt["stg"] += 1
                gi = last_sg
                P.op("dve", lambda e, pz=pz, si=si, gi=gi: e.tensor_tensor(out=stg[si], in0=pz, in1=sg[gi],
                                                                           op=ALU.mult),
                     r=[f"bank{pb}", f"sg{gi}"], w=[f"stg{si}"])
                P.dma("sp", io["yT"][row0:row0 + 128, tsl], stg[si], r=[f"stg{si}"], w=["o_y"],
                      chan=f"of{si}", final=True)
            elif kind in ("r", "g"):
                si = cnt["stg"] % 4
                cnt["stg"] += 1
                if kind == "r":
                    P.op("act", lambda e, pz=pz, si=si: e.copy(out=stg[si], in_=pz),
                         r=[f"bank{pb}"], w=[f"stg{si}"])
                else:
                    P.op("act", lambda e, pz=pz, si=si: e.activation(out=stg[si], in_=pz,
                                                                     func=AF.Gelu_apprx_tanh),
                         r=[f"bank{pb}"], w=[f"stg{si}"])
                dst = io["xrT" if kind == "r" else "ggT"][row0:row0 + 128, tsl]
                P.dma("sp", dst, stg[si], r=[f"stg{si}"], w=["o_" + kind], chan=f"of{si}", final=True)
            else:
                si = cnt["stg"] % 4
                cnt["stg"] += 1
                P.op("act", lambda e, pz=pz, si=si: e.activation(out=stg[si][0:8, :], in_=pz[0:8, :], func=AF.Exp,
                                                                 scale=-1.0, bias=nbfg[0:8, :]),
                     r=[f"bank{pb}", "nbfg"], w=[f"stg{si}"])
                P.op("act", lambda e, si=si, tsl=tsl: e.activation(out=lf[0:8, tsl], in_=stg[si][0:8, :],
                                                                   func=AF.Ln, scale=1.0, bias=1.0),
                     r=[f"stg{si}"], w=["lf"])
        for tt in range(4):
            pb = 4 + (tt % 2)
            pv = banks[pb]
            for kc in range(8):
                P.op("pe", lambda e, kc=kc, pv=pv, hs=hs, tt=tt: e.matmul(
                    out=pv, lhsT=hT[hs][:, kc, 128 * tt:128 * (tt + 1)], rhs=w_sb[:, kc, V0:V0 + 512],
                    start=(kc == 0), stop=(kc == 7)),
                    r=["w_sb", f"hT{hs}"], w=[f"bank{pb}"])
            sb_i = cnt["stb"] % 4
            cnt["stb"] += 1
            P.op("dve", lambda e, pv=pv, sb_i=sb_i: e.tensor_copy(out=stb[sb_i], in_=pv),
                 r=[f"bank{pb}"], w=[f"stb{sb_i}"])
            i = 4 * c + tt
            P.dma("sp", io["v"][128 * i:128 * (i + 1), :], stb[sb_i], r=[f"stb{sb_i}"], w=["o_v"],
                  chan=f"ob{sb_i}", final=True)

    P.op("dve", lambda e: e.tensor_tensor_scan(out=cum[0:8, :], data0=onecol[0:8, 0:1].to_broadcast([8, T]),
                                               data1=lf[0:8, :], initial=0.0, op0=ALU.mult, op1=ALU.subtract),
         r=["lf", "onecol"], w=["cum"])
    P.op("dve", lambda e: e.tensor_scalar(out=lf[0:8, :], in0=cum[0:8, :], scalar1=8.0, scalar2=None, op0=ALU.mult),
         r=["cum"], w=["lf"])
    for j in range(3):
        P.op("dve", lambda e, j=j: e.tensor_copy(out=cq3[0:8, j, :], in_=lf[0:8, :]), r=["lf"], w=["cq3"])
        if j < 2:
            P.op("dve", lambda e, j=j: e.tensor_tensor(out=lf[0:8, :], in0=lf[0:8, :], in1=cq3[0:8, j, :],
                                                       op=ALU.subtract), r=["lf", "cq3"], w=["lf"])
    P.dma("sp", io["cq3"], cq3[0:8, :, :], r=["cq3"], w=["o_cq3"], chan="c2", final=True)
    for which in range(2):
        pb = 4 + which
        tpc = banks[pb][:, 0:NB * 8].rearrange("p (a b) -> p a b", b=8)
        if which == 0:
            src = cum
            rk = "cum"
        else:
            P.op("dve", lambda e: e.tensor_scalar(out=lf[0:8, :], in0=cum[0:8, :], scalar1=cum[0:8, T - 1:T],
                                                  scalar2=None, op0=ALU.subtract), r=["cum"], w=["lf"])
            src = lf
            rk = "lf"
        for b in range(NB):
            P.op("pe", lambda e, b=b, tpc=tpc, src=src: e.transpose(
                out=tpc[:, b, :], in_=src[0:8, 128 * b:128 * (b + 1)], identity=C.identf[0:8, 0:8]),
                r=[rk, "identf"], w=[f"bank{pb}"])
        P.op("act", lambda e, tpc=tpc: e.activation(out=kbs, in_=tpc, func=AF.Copy, scale=-1.0),
             r=[f"bank{pb}"], w=["kbs"])
        P.dma("sp", io["kb_own" if which == 0 else "kb_next"], kbs, r=["kbs"], w=[f"o_kb{which}"],
              chan="c3", final=True)
    A.release()


P1_OUT = lambda T: {
    "qT": ([512, T], BF16), "kT": ([512, T], BF16), "v": ([T, 512], BF16), "cq3": ([8, 3, T], BF16),
    "kb_own": ([128, T // 128, 8], F32), "kb_next": ([128, T // 128, 8], F32),
    "yT": ([256, T], F32), "xrT": ([256, T], F32), "ggT": ([256, T], F32),
}


def build_k1(T):
    nc = bass.Bass("TRN2", target_bir_lowering=False)
    P = Prog(nc)
    io = {
        "x": _dram(nc, "x", [T, D], F32, "ExternalInput"),
        "w_in": _dram(nc, "w_in", [D, N_IN], F32, "ExternalInput"),
        "g1": _dram(nc, "g1", [1, D], F32, "ExternalInput"),
        "bfg": _dram(nc, "bfg", [8, 1], F32, "ExternalInput"),
    }
    for k, (shp, dt) in P1_OUT(T).items():
        io[k] = _dram(nc, k, shp, dt, "ExternalOutput")
    A = Arena(P, 190 * 1024)
    banks = [P.ps(f"bank{i}", [128, 512], F32)[:, :] for i in range(8)]
    C = setup_common(P, A, banks)
    phase1(P, A, banks, C, T, io)
    P.emit()
    return nc


def phase_rnn(P, A, banks, C, T, io, mixT, flags, has_prev=True):
    A.mark()
    SEG = min(1024, T)
    nseg = 2 * T // SEG
    xe = A.alloc([3 + SEG], F32)
    xc = A.alloc([SEG], F32)
    xcb = A.alloc([SEG], BF16)
    rr = A.alloc([SEG], F32)
    ii = A.alloc([SEG], F32)
    aa = A.alloc([SEG], F32)
    mm = A.alloc([SEG], F32)
    hsb = A.alloc([SEG], F32)
    ggb = A.alloc([SEG], F32)
    rwt = A.alloc([4], F32)
    rpt = A.alloc([4], F32)
    cneg = A.alloc([2], F32)
    carry = A.alloc([1], F32)
    wbf = A.alloc([128], F32)
    wbb = [A.alloc([128], BF16) for _ in range(2)]
    for cc in range(2):
        rows = slice(128 * cc, 128 * (cc + 1))
        P.dma("sp", rwt, io["rw"][rows, :], w=["rwt"], chan="r0")
        P.dma("sp", rpt, io["rpar"][rows, :], w=["rpt"], chan="r1")
        P.op("act", lambda e: e.activation(out=cneg[:, 0:1], in_=rpt[:, 3:4], func=AF.Exp, scale=-1.0),
             r=["rpt"], w=["cneg"])
        P.op("act", lambda e: e.activation(out=cneg[:, 0:1], in_=cneg[:, 0:1], func=AF.Ln, scale=1.0, bias=1.0),
             r=["cneg"], w=["cneg"])
        P.op("dve", lambda e: e.tensor_scalar(out=cneg[:, 1:2], in0=cneg[:, 0:1], scalar1=-16.0, scalar2=None,
                                              op0=ALU.mult), r=["cneg"], w=["cneg"])
        P.op("dve", lambda e: e.tensor_scalar(out=cneg[:, 0:1], in0=cneg[:, 0:1], scalar1=-8.0, scalar2=None,
                                              op0=ALU.mult), r=["cneg"], w=["cneg"])
        for wi, nm in enumerate(("wr", "wi")):
            P.op("pool", lambda e: e.memset(wbf, 0.0), w=["wbf"])
            P.dma("sp", wbf[0:64, 0:64], io[nm][2 * cc], w=["wbf"], chan="r2")
            P.dma("sp", wbf[64:128, 64:128], io[nm][2 * cc + 1], w=["wbf"], chan="r3")
            P.op("dve", lambda e, wi=wi: e.tensor_copy(out=wbb[wi], in_=wbf), r=["wbf"], w=[f"wbb{wi}"])
        P.op("pool", lambda e: e.memset(carry, 0.0), w=["carry"])
        for s in range(nseg):
            t0 = s * SEG
            if not has_prev and t0 < T:
                continue
            if not has_prev and t0 == T:
                P.op("pool", lambda e: e.memset(xe[:, 0:3], 0.0), w=["xe"])
                P.dma("sp", xe[:, 3:3 + SEG], io["xr_own"][rows, 0:SEG], w=["xe"], chan="r4")
            elif s == 0:
                P.op("pool", lambda e: e.memset(xe[:, 0:3], 0.0), w=["xe"])
                P.dma("sp", xe[:, 3:3 + SEG], io["xr_prev"][rows, 0:SEG], w=["xe"], chan="r4")
            elif t0 < T:
                P.dma("sp", xe, io["xr_prev"][rows, t0 - 3:t0 + SEG], w=["xe"], chan="r4")
            elif t0 == T:
                P.dma("sp", xe[:, 0:3], io["xr_prev"][rows, T - 3:T], w=["xe"], chan="r4")
                P.dma("sp", xe[:, 3:3 + SEG], io["xr_own"][rows, 0:SEG], w=["xe"], chan="r4")
            else:
                P.dma("sp", xe, io["xr_own"][rows, t0 - T - 3:t0 - T + SEG], w=["xe"], chan="r4")
            if t0 == T:
                P.op("dve", lambda e: e.tensor_scalar(out=xe[:, 0:3], in0=xe[:, 0:3], scalar1=flags[:, 1:2],
                                                      scalar2=None, op0=ALU.mult), r=["xe", "flags"], w=["xe"])
            P.op("dve", lambda e: e.tensor_scalar(out=xc, in0=xe[:, 3:3 + SEG], scalar1=rwt[:, 3:4],
                                                  scalar2=rpt[:, 0:1], op0=ALU.mult, op1=ALU.add),
                 r=["xe", "rwt", "rpt"], w=["xc"])
            for k in range(3):
                P.op("dve", lambda e, k=k: e.scalar_tensor_tensor(out=xc, in0=xe[:, k:k + SEG], scalar=rwt[:, k:k + 1],
                                                                   in1=xc, op0=ALU.mult, op1=ALU.add),
                     r=["xe", "rwt", "xc"], w=["xc"])
            P.op("act", lambda e: e.copy(out=xcb, in_=xc), r=["xc"], w=["xcb"])
            for ch in range(SEG // 512):
                sl = slice(512 * ch, 512 * (ch + 1))
                for wi, (dst, bcol) in enumerate(((rr, 1), (ii, 2))):
                    pb = 2 * (ch % 2) + wi
                    P.op("pe", lambda e, wi=wi, pb=pb, sl=sl: e.matmul(out=banks[pb], lhsT=wbb[wi], rhs=xcb[:, sl],
                                                                        start=True, stop=True),
                         r=[f"wbb{wi}", "xcb"], w=[f"bank{pb}"])
                    P.op("act", lambda e, dst=dst, pb=pb, sl=sl, bcol=bcol: e.activation(
                        out=dst[:, sl], in_=banks[pb], func=AF.Sigmoid, bias=rpt[:, bcol:bcol + 1], scale=1.0),
                        r=[f"bank{pb}", "rpt"], w=["rr" if wi == 0 else "ii"])
            P.op("act", lambda e: e.activation(out=aa, in_=rr, func=AF.Exp, scale=cneg[:, 0:1]),
                 r=["rr", "cneg"], w=["aa"])
            P.op("act", lambda e: e.activation(out=mm, in_=rr, func=AF.Exp, scale=cneg[:, 1:2]),
                 r=["rr", "cneg"], w=["mm"])
            P.op("act", lambda e: e.activation(out=mm, in_=mm, func=AF.Sqrt, scale=-1.0, bias=1.0),
                 r=["mm"], w=["mm"])
            P.op("dve", lambda e: e.tensor_tensor(out=ii, in0=ii, in1=mm, op=ALU.mult), r=["ii", "mm"], w=["ii"])
            P.op("dve", lambda e: e.tensor_tensor(out=ii, in0=ii, in1=xc, op=ALU.mult), r=["ii", "xc"], w=["ii"])
            if t0 == T:
                P.op("dve", lambda e: e.tensor_scalar(out=aa[:, 0:1], in0=aa[:, 0:1], scalar1=flags[:, 1:2],
                                                      scalar2=None, op0=ALU.mult), r=["aa", "flags"], w=["aa"])
            P.op("dve", lambda e: e.tensor_tensor_scan(out=hsb, data0=aa, data1=ii, initial=carry[:, 0:1],
                                                       op0=ALU.mult, op1=ALU.add),
                 r=["aa", "ii", "carry"], w=["hsb"])
            P.op("dve", lambda e: e.tensor_copy(out=carry, in_=hsb[:, SEG - 1:SEG]), r=["hsb"], w=["carry"])
            if t0 >= T:
                o0 = t0 - T
                P.dma("sp", ggb, io["ggT"][rows, o0:o0 + SEG], w=["ggb"], chan="r5")
                P.op("dve", lambda e, o0=o0, cc=cc: e.tensor_tensor(out=mixT[:, 6 + cc, o0:o0 + SEG], in0=hsb, in1=ggb,
                                                                     op=ALU.mult), r=["hsb", "ggb"], w=["mixT"])
    A.release()


def phase_conv(P, A, banks, C, T, io, mixT, flags):
    A.mark()
    ye = [A.alloc([30 + T], F32) for _ in range(2)]
    acc = [A.alloc([T], F32) for _ in range(2)]
    cwt = [A.alloc([31], F32) for _ in range(2)]
    cpt = [A.alloc([3], F32) for _ in range(2)]
    onesm = A.alloc([128], F32)
    sq = [A.alloc([512], F32) for _ in range(2)]
    mt = A.alloc([512], F32)
    vt = A.alloc([512], F32)
    zt = A.alloc([512], F32)
    P.op("pool", lambda e: e.memset(onesm, 1.0 / 256.0), w=["onesm"])
    for cc in range(2):
        rows = slice(128 * cc, 128 * (cc + 1))
        P.dma("sp", ye[cc][:, 0:30], io["y_prev"][rows, T - 30:T], w=[f"ye{cc}"], chan=f"v0{cc}")
        P.dma("sp", ye[cc][:, 30:30 + T], io["y_own"][rows, :], w=[f"ye{cc}"], chan=f"v0{cc}")
        P.dma("sp", cwt[cc], io["cw"][rows, :], w=[f"cwt{cc}"], chan="v1")
        P.dma("sp", cpt[cc], io["cpar"][rows, :], w=[f"cpt{cc}"], chan="v2")
        P.op("dve", lambda e, cc=cc: e.tensor_scalar(out=ye[cc][:, 0:30], in0=ye[cc][:, 0:30], scalar1=flags[:, 1:2],
                                                     scalar2=None, op0=ALU.mult), r=[f"ye{cc}", "flags"], w=[f"ye{cc}"])
        P.op("dve", lambda e, cc=cc: e.tensor_scalar(out=acc[cc], in0=ye[cc][:, 0:T], scalar1=cwt[cc][:, 0:1],
                                                     scalar2=cpt[cc][:, 0:1], op0=ALU.mult, op1=ALU.add),
             r=[f"ye{cc}", f"cwt{cc}", f"cpt{cc}"], w=[f"acc{cc}"])
        for k in range(1, 31):
            P.op("dve", lambda e, cc=cc, k=k: e.scalar_tensor_tensor(out=acc[cc], in0=ye[cc][:, k:k + T],
                                                                      scalar=cwt[cc][:, k:k + 1], in1=acc[cc],
                                                                      op0=ALU.mult, op1=ALU.add),
                 r=[f"ye{cc}", f"cwt{cc}", f"acc{cc}"], w=[f"acc{cc}"])
    for ch in range(T // 512):
        sl = slice(512 * ch, 512 * (ch + 1))
        for cc in range(2):
            P.op("act", lambda e, cc=cc, sl=sl: e.activation(out=sq[cc], in_=acc[cc][:, sl], func=AF.Square),
                 r=[f"acc{cc}"], w=[f"sq{cc}"])
        for cc in range(2):
            P.op("pe", lambda e, cc=cc, sl=sl: e.matmul(out=banks[4], lhsT=onesm, rhs=acc[cc][:, sl],
                                                        start=(cc == 0), stop=(cc == 1)),
                 r=["onesm", f"acc{cc}"], w=["bank4"])
        for cc in range(2):
            P.op("pe", lambda e, cc=cc: e.matmul(out=banks[5], lhsT=onesm, rhs=sq[cc],
                                                 start=(cc == 0), stop=(cc == 1)),
                 r=["onesm", f"sq{cc}"], w=["bank5"])
        P.op("act", lambda e: e.copy(out=mt, in_=banks[4]), r=["bank4"], w=["mt"])
        P.op("dve", lambda e: e.tensor_tensor(out=vt, in0=mt, in1=mt, op=ALU.mult), r=["mt"], w=["vt"])
        P.op("dve", lambda e: e.tensor_tensor(out=vt, in0=banks[5], in1=vt, op=ALU.subtract), r=["bank5", "vt"], w=["vt"])
        P.op("act", lambda e: e.activation(out=vt, in_=vt, func=AF.Sqrt, scale=1.0, bias=EPS), r=["vt"], w=["vt"])
        P.op("dve", lambda e: e.reciprocal(out=vt, in_=vt), r=["vt"], w=["vt"])
        for cc in range(2):
            P.op("dve", lambda e, cc=cc, sl=sl: e.tensor_tensor(out=zt, in0=acc[cc][:, sl], in1=mt, op=ALU.subtract),
                 r=[f"acc{cc}", "mt"], w=["zt"])
            P.op("dve", lambda e: e.tensor_tensor(out=zt, in0=zt, in1=vt, op=ALU.mult), r=["zt", "vt"], w=["zt"])
            P.op("act", lambda e, cc=cc, sl=sl: e.activation(out=mixT[:, 4 + cc, sl], in_=zt, func=AF.Silu,
                                                             scale=cpt[cc][:, 1:2], bias=cpt[cc][:, 2:3]),
                 r=["zt", f"cpt{cc}"], w=["mixT"])
    A.release()


def phase_attn(P, A, banks, C, T, io, mixT, flags, has_prev=True):
    A.mark()
    NB = T // 128
    NB2 = 2 * NB
    NCH = T // 512
    KA = [A.alloc([2 * T], BF16) for _ in range(2)]
    QA = [A.alloc([T], BF16) for _ in range(2)]
    VA = [A.alloc([NB2, 128], BF16) for _ in range(2)]
    kbm = A.alloc([NB2, 8], F32)
    PT = [A.alloc([512], BF16) for _ in range(3)]
    Osb = A.alloc([512], F32)
    rrow = A.alloc([512], F32)
    for s in range(2):
        P.op("pool", lambda e, s=s: e.memset(KA[s][64:67, :], 1.0), w=[f"KA{s}"])
    P.op("pool", lambda e: e.memset(VA[0][:, :, 64:128], 1.0), w=["VA0"])
    P.op("pool", lambda e: e.memset(VA[1][:, :, 0:64], 1.0), w=["VA1"])
    P.dma("sp", kbm[:, 0:NB, :], io["kb_prev"], w=["kbm"], chan="a0")
    P.dma("sp", kbm[:, NB:NB2, :], io["kb_own"], w=["kbm"], chan="a0")
    P.op("dve", lambda e: e.tensor_scalar(out=kbm[:, 0:NB, :], in0=kbm[:, 0:NB, :], scalar1=flags[:, 0:1],
                                          scalar2=None, op0=ALU.add), r=["kbm", "flags"], w=["kbm"])
    vpv = io["v_prev"].rearrange("(b p) c -> p b c", p=128)
    vov = io["v_own"].rearrange("(b p) c -> p b c", p=128)

    def load_head(h):
        s = h % 2
        if has_prev:
            P.dma("sp", KA[s][0:64, 0:T], io["kT_prev"][64 * h:64 * (h + 1), :], w=[f"KA{s}"], chan=f"ka{s}")
        P.dma("sp", KA[s][0:64, T:2 * T], io["kT_own"][64 * h:64 * (h + 1), :], w=[f"KA{s}"], chan=f"ka{s}")
        P.dma("sp", QA[s][0:64, :], io["qT"][64 * h:64 * (h + 1), :], w=[f"QA{s}"], chan=f"qa{s}")
        P.dma("sp", QA[s][64:67, :], io["cq3"][h], w=[f"QA{s}"], chan=f"qb{s}")
        c0 = 0 if s == 0 else 64
        if has_prev:
            P.dma("sp", VA[s][:, 0:NB, c0:c0 + 64], vpv[:, :, 64 * h:64 * (h + 1)], w=[f"VA{s}"], chan=f"va{s}")
        P.dma("sp", VA[s][:, NB:NB2, c0:c0 + 64], vov[:, :, 64 * h:64 * (h + 1)], w=[f"VA{s}"], chan=f"va{s}")

    tiles = []
    for h in range(8):
        for qc in range(NCH):
            blks = [(g, None) for g in range(NB)] if has_prev else []
            for ob in range(4 * qc + 4):
                blks.append((NB + ob, (ob - 4 * qc) if ob >= 4 * qc else None))
            for bi, (g, j) in enumerate(blks):
                tiles.append(dict(h=h, qc=qc, g=g, j=j, first=(bi == 0), last=(bi == len(blks) - 1),
                                  grp=h * NCH + qc, lasth=(qc == NCH - 1 and bi == len(blks) - 1)))
    SK = 2
    load_head(0)
    load_head(1)
    for i in range(len(tiles) + SK):
        if i < len(tiles):
            t = tiles[i]
            h, qc, g, j = t["h"], t["qc"], t["g"], t["j"]
            s = h % 2
            c0 = 0 if j is None else 128 * j
            zb = i % 3
            Z = banks[zb]
            P.op("pe", lambda e, Z=Z, s=s, g=g, qc=qc, c0=c0, j=j: e.matmul(
                out=Z[:, c0:512], lhsT=KA[s][0:67, 128 * g:128 * (g + 1)],
                rhs=QA[s][0:67, 512 * qc + c0:512 * (qc + 1)], start=True, stop=(j is None)),
                r=[f"KA{s}", f"QA{s}"], w=[f"bank{zb}"])
            if j is not None:
                P.op("pe", lambda e, Z=Z, c0=c0: e.matmul(out=Z[:, c0:c0 + 128], lhsT=C.identb, rhs=C.maskT,
                                                           start=False, stop=True),
                     r=["identb", "maskT"], w=[f"bank{zb}"])
            P.op("act", lambda e, Z=Z, zb=zb, c0=c0, g=g, h=h: e.activation(
                out=PT[zb][:, c0:512], in_=Z[:, c0:512], func=AF.Exp, scale=0.125, bias=kbm[:, g, h:h + 1]),
                r=[f"bank{zb}", "kbm"], w=[f"PT{zb}"])
        if i >= SK:
            t = tiles[i - SK]
            h, qc, g, j = t["h"], t["qc"], t["g"], t["j"]
            s = h % 2
            c0 = 0 if j is None else 128 * j
            zb = (i - SK) % 3
            ob = 3 + (t["grp"] % 2)
            O = banks[ob]
            P.op("pe", lambda e, O=O, s=s, g=g, zb=zb, c0=c0, t=t: e.matmul(
                out=O[:, c0:512], lhsT=VA[s][:, g, :], rhs=PT[zb][:, c0:512], start=t["first"], stop=t["last"]),
                r=[f"VA{s}", f"PT{zb}"], w=[f"bank{ob}"])
            if t["last"]:
                p0 = 64 if s == 0 else 0
                r0 = 0 if s == 0 else 64
                P.op("act", lambda e, O=O: e.copy(out=Osb, in_=O), r=[f"bank{ob}"], w=["Osb"])
                P.op("dve", lambda e, p0=p0: e.reciprocal(out=rrow[p0:p0 + 1, :], in_=Osb[p0:p0 + 1, :]),
                     r=["Osb"], w=["rrow"])
                P.op("pe", lambda e, p0=p0: e.matmul(out=banks[5], lhsT=C.onesf[p0:p0 + 1, :], rhs=rrow[p0:p0 + 1, :],
                                                     start=True, stop=True), r=["onesf", "rrow"], w=["bank5"])
                P.op("dve", lambda e, r0=r0, h=h, qc=qc: e.tensor_tensor(
                    out=mixT[r0:r0 + 64, h // 2, 512 * qc:512 * (qc + 1)], in0=Osb[r0:r0 + 64, :],
                    in1=banks[5][r0:r0 + 64, :], op=ALU.mult), r=["Osb", "bank5"], w=["mixT"])
            if t["lasth"] and h + 2 < 8:
                load_head(h + 2)
    A.release()


NG = 16
ND = 4


def phase_wout(P, A, banks, C, T, io, mixT):
    A.mark()
    NT = T // 128
    wo = A.alloc([8, D], BF16)
    xt = [A.alloc([D], F32) for _ in range(2)]
    xs1 = [A.alloc([D], F32) for _ in range(2)]
    wov = io["w_out"].rearrange("(kc p) n -> p kc n", p=128)
    for kc in range(8):
        P.dma("pool", wo[:, kc, :], wov[:, kc, :], w=["wo"], chan=f"pw{kc % 2}")
    xv = io["x"].rearrange("(n p) d -> n p d", p=128)
    x1v = io["x1"].rearrange("(n p) d -> n p d", p=128)
    for i in range(NT):
        xs = i % 2
        tsl = slice(128 * i, 128 * (i + 1))
        P.dma("sp", xt[xs], xv[i], w=[f"xt{xs}"], chan=f"px{xs}")
        for hf in range(2):
            pb = 2 * xs + hf
            for kc in range(8):
                P.op("pe", lambda e, hf=hf, kc=kc, tsl=tsl, pb=pb: e.matmul(
                    out=banks[pb], lhsT=mixT[:, kc, tsl], rhs=wo[:, kc, 512 * hf:512 * (hf + 1)],
                    start=(kc == 0), stop=(kc == 7)), r=["mixT", "wo"], w=[f"bank{pb}"])
            P.op("dve", lambda e, hf=hf, xs=xs, pb=pb: e.tensor_tensor(
                out=xs1[xs][:, 512 * hf:512 * (hf + 1)], in0=xt[xs][:, 512 * hf:512 * (hf + 1)], in1=banks[pb],
                op=ALU.add), r=[f"xt{xs}", f"bank{pb}"], w=[f"xs1{xs}"])
        P.dma("sp", x1v[i], xs1[xs], r=[f"xs1{xs}"], w=["o_x1"], chan=f"py{xs}")
    A.release()


def phase_peer(P, A, banks, C, T, io, last_layer):
    A.mark()
    NT = T // 128
    wqs = A.alloc([8, 2048], BF16)
    kks = A.alloc([16, 128], BF16)
    g2b = A.alloc([D], F32)
    gfb = A.alloc([D], F32) if last_layer else None
    x1 = [A.alloc([D], F32) for _ in range(3)]
    h2f = [A.alloc([D], F32) for _ in range(3)]
    eidx = [A.alloc([128], I32) for _ in range(2)]
    gate = [A.alloc([8, 16], F32) for _ in range(2)]
    sqj = A.alloc([D], F32)
    junkb = A.alloc([D], F32)
    ot = A.alloc([D], F32)
    ss = A.alloc([2], F32)
    h2b = A.alloc([D], BF16)
    h2T = A.alloc([8, 128], BF16)
    qTb = A.alloc([16, 128], BF16)
    scs2 = [A.alloc([16, 128], F32) for _ in range(2)]
    sc2 = A.alloc([128], F32)
    tv = A.alloc([16, 16], F32)
    ti = A.alloc([16, 16], U32)
    tif = A.alloc([16, 16], F32)
    cand = A.alloc([8, 256], F32)
    cand2 = A.alloc([256], F32)
    sv = A.alloc([8, 16], F32)
    ci = A.alloc([8, 16], U32)
    irow = A.alloc([8, 16], U32)
    jcol = A.alloc([8, 16], U32)
    irf = A.alloc([8, 16], F32)
    jcf = A.alloc([8, 16], F32)
    e1 = A.alloc([8, 16], F32)
    e2 = A.alloc([8, 16], F32)
    iot_i = A.alloc([16], I32)
    iot = A.alloc([16], F32)
    gsum = A.alloc([8], F32)
    hid = A.alloc([128], F32)
    wgt = A.alloc([128], F32)
    gb = [A.alloc([2 * D], BF16) for _ in range(NG)]
    dg = [A.alloc([128], BF16) for _ in range(ND)]
    gel = A.alloc([128], F32)

    wqv = io["wq"].rearrange("(kc p) n -> p kc n", p=128)
    for kc in range(8):
        P.dma("pool", wqs[:, kc, :], wqv[:, kc, :], w=["wqs"], chan=f"pq{kc % 2}")
    P.dma("pool", kks, io["kk"], w=["kks"], chan="pk")
    P.dma("sp", g2b, io["g2"].to_broadcast([128, D]), w=["g2b"], chan="p0")
    if last_layer:
        P.dma("sp", gfb, io["gf"].to_broadcast([128, D]), w=["gfb"], chan="p1")
    P.op("pool", lambda e: e.iota(out=iot_i, pattern=[[1, 16]], base=0, channel_multiplier=0), w=["iot_i"])
    P.op("dve", lambda e: e.tensor_copy(out=iot, in_=iot_i), r=["iot_i"], w=["iot"])

    xv = io["x1"].rearrange("(n p) d -> n p d", p=128)
    ov = io["xo"].rearrange("(n p) d -> n p d", p=128)
    tvv = tv.rearrange("p (h s) k -> p h s k", s=2)
    tifv = tif.rearrange("p (h s) k -> p h s k", s=2)
    cand4 = cand.rearrange("p h (i j) -> p h i j", j=16)
    msk = cand4
    iob = iot.unsqueeze(1).unsqueeze(1).to_broadcast([128, 8, 16, 16])
    tpv = banks[2].bitcast(BF16).rearrange("p (a b) -> p a b", b=128)

    def stage_a1(i):
        s3 = i % 3
        X1, H2F = x1[s3], h2f[s3]
        kx, kh = f"x1{s3}", f"h2f{s3}"
        scs = scs2[i % 2]
        ks = f"scs{i % 2}"
        P.dma("sp", X1, xv[i], w=[kx], chan=f"px{s3}")
        P.op("act", lambda e: e.activation(out=sqj, in_=X1, func=AF.Square, accum_out=ss[:, 0:1]),
             r=[kx], w=["ss0"])
        P.op("act", lambda e: e.activation(out=ss[:, 0:1], in_=ss[:, 0:1], func=AF.Sqrt, scale=1.0 / D, bias=EPS),
             r=["ss0"], w=["ss0"])
        P.op("dve", lambda e: e.reciprocal(out=ss[:, 0:1], in_=ss[:, 0:1]), r=["ss0"], w=["ss0"])
        P.op("dve", lambda e: e.scalar_tensor_tensor(out=H2F, in0=X1, scalar=ss[:, 0:1], in1=g2b,
                                                      op0=ALU.mult, op1=ALU.mult), r=[kx, "ss0", "g2b"], w=[kh])
        P.op("act", lambda e: e.copy(out=h2b, in_=H2F), r=[kh], w=["h2b"])
        for kc in range(8):
            P.op("pe", lambda e, kc=kc: e.transpose(out=tpv[:, kc, :], in_=h2b[:, 128 * kc:128 * (kc + 1)],
                                                    identity=C.identb), r=["h2b", "identb"], w=["bank2"])
        P.op("act", lambda e: e.copy(out=h2T, in_=tpv), r=["bank2"], w=["h2T"])
        for qg in range(4):
            pb = 3 + (qg % 2)
            qps = banks[pb].rearrange("p (a b) -> p a b", b=128)
            for jj in range(4):
                j = 4 * qg + jj
                for kc in range(8):
                    P.op("pe", lambda e, qps=qps, jj=jj, j=j, kc=kc: e.matmul(
                        out=qps[:, jj, :], lhsT=wqs[:, kc, 128 * j:128 * (j + 1)], rhs=h2T[:, kc, :],
                        start=(kc == 0), stop=(kc == 7)), r=["wqs", "h2T"], w=[f"bank{pb}"])
            P.op("act", lambda e, qps=qps, qg=qg: e.copy(out=qTb[:, 4 * qg:4 * qg + 4, :], in_=qps),
                 r=[f"bank{pb}"], w=["qTb"])
        for qg in range(4):
            pb = 5 + (qg % 2)
            sps = banks[pb].rearrange("p (a b) -> p a b", b=128)
            for jj in range(4):
                j = 4 * qg + jj
                P.op("pe", lambda e, sps=sps, jj=jj, j=j: e.matmul(out=sps[:, jj, :], lhsT=qTb[:, j, :],
                                                                    rhs=kks[:, j, :], start=True, stop=True),
                     r=["qTb", "kks"], w=[f"bank{pb}"])
            P.op("act", lambda e, sps=sps, qg=qg: e.copy(out=scs[:, 4 * qg:4 * qg + 4, :], in_=sps),
                 r=[f"bank{pb}"], w=[ks])

    def stage_a2(i):
        sl = i % 2
        EIDX, GATE = eidx[sl], gate[sl]
        ke, kg = f"eidx{sl}", f"gate{sl}"
        scs = scs2[i % 2]
        ks = f"scs{i % 2}"
        for j in range(16):
            P.op("dve", lambda e, j=j: e.max(out=tv[:, j, 0:8], in_=scs[:, j, :]), r=[ks], w=["tv"])
            P.op("dve", lambda e, j=j: e.max_index(out=ti[:, j, 0:8], in_max=tv[:, j, 0:8], in_values=scs[:, j, :]),
                 r=[ks, "tv"], w=["ti"])
            P.op("dve", lambda e, j=j: e.match_replace(out=sc2, in_to_replace=tv[:, j, 0:8], in_values=scs[:, j, :],
                                                       imm_value=-1e30), r=[ks, "tv"], w=["sc2"])
            P.op("dve", lambda e, j=j: e.max(out=tv[:, j, 8:16], in_=sc2), r=["sc2"], w=["tv"])
            P.op("dve", lambda e, j=j: e.max_index(out=ti[:, j, 8:16], in_max=tv[:, j, 8:16], in_values=sc2),
                 r=["sc2", "tv"], w=["ti"])
        P.op("dve", lambda e: e.tensor_tensor(
            out=cand4, in0=tvv[:, :, 0, :].unsqueeze(3).to_broadcast([128, 8, 16, 16]),
            in1=tvv[:, :, 1, :].unsqueeze(2).to_broadcast([128, 8, 16, 16]), op=ALU.add), r=["tv"], w=["cand"])
        for h in range(8):
            P.op("dve", lambda e, h=h: e.max(out=sv[:, h, 0:8], in_=cand[:, h, :]), r=["cand"], w=["sv"])
            P.op("dve", lambda e, h=h: e.max_index(out=ci[:, h, 0:8], in_max=sv[:, h, 0:8], in_values=cand[:, h, :]),
                 r=["cand", "sv"], w=["ci"])
            P.op("dve", lambda e, h=h: e.match_replace(out=cand2, in_to_replace=sv[:, h, 0:8], in_values=cand[:, h, :],
                                                       imm_value=-1e30), r=["cand", "sv"], w=["cand2"])
            P.op("dve", lambda e, h=h: e.max(out=sv[:, h, 8:16], in_=cand2), r=["cand2"], w=["sv"])
            P.op("dve", lambda e, h=h: e.max_index(out=ci[:, h, 8:16], in_max=sv[:, h, 8:16], in_values=cand2),
                 r=["cand2", "sv"], w=["ci"])
        P.op("dve", lambda e: e.tensor_single_scalar(out=irow, in_=ci, scalar=4, op=ALU.logical_shift_right),
             r=["ci"], w=["irow"])
        P.op("dve", lambda e: e.tensor_single_scalar(out=jcol, in_=ci, scalar=15, op=ALU.bitwise_and),
             r=["ci"], w=["jcol"])
        P.op("dve", lambda e: e.tensor_copy(out=irf, in_=irow), r=["irow"], w=["irf"])
        P.op("dve", lambda e: e.tensor_copy(out=jcf, in_=jcol), r=["jcol"], w=["jcf"])
        P.op("dve", lambda e: e.tensor_copy(out=tif, in_=ti), r=["ti"], w=["tif"])
        for (src, half, dst, dk) in ((irf, 0, e1, "e1"), (jcf, 1, e2, "e2")):
            P.op("dve", lambda e, src=src: e.tensor_tensor(
                out=msk, in0=src.unsqueeze(3).to_broadcast([128, 8, 16, 16]), in1=iob, op=ALU.is_equal),
                r=["irf", "jcf", "iot"], w=["cand"])
            P.op("dve", lambda e, half=half: e.tensor_tensor(
                out=msk, in0=msk, in1=tifv[:, :, half, :].unsqueeze(2).to_broadcast([128, 8, 16, 16]),
                op=ALU.mult), r=["cand", "tif"], w=["cand"])
            P.op("dve", lambda e, dst=dst: e.tensor_reduce(out=dst, in_=msk, axis=AX.X, op=ALU.add),
                 r=["cand"], w=[dk])
        P.op("dve", lambda e: e.scalar_tensor_tensor(out=e1, in0=e1, scalar=128.0, in1=e2, op0=ALU.mult, op1=ALU.add),
             r=["e1", "e2"], w=["e1"])
        P.op("dve", lambda e: e.tensor_copy(out=EIDX, in_=e1.rearrange("p h k -> p (h k)")), r=["e1"], w=[ke])
        P.op("dve", lambda e: e.tensor_tensor(out=GATE, in0=sv, in1=sv[:, :, 0:1].to_broadcast([128, 8, 16]),
                                              op=ALU.subtract), r=["sv"], w=[kg])
        P.op("act", lambda e: e.activation(out=GATE, in_=GATE, func=AF.Exp), r=[kg], w=[kg])
        P.op("dve", lambda e: e.tensor_reduce(out=gsum, in_=GATE, axis=AX.X, op=ALU.add), r=[kg], w=["gsum"])
        P.op("dve", lambda e: e.reciprocal(out=gsum, in_=gsum), r=["gsum"], w=["gsum"])
        P.op("dve", lambda e: e.tensor_tensor(out=GATE, in0=GATE, in1=gsum.unsqueeze(2).to_broadcast([128, 8, 16]),
                                              op=ALU.mult), r=[kg, "gsum"], w=[kg])

    def stage_b(i):
        sl = i % 2
        s3 = i % 3
        X1, H2F, EIDX, GATE = x1[s3], h2f[s3], eidx[sl], gate[sl]
        kx, kh, ke, kg = f"x1{s3}", f"h2f{s3}", f"eidx{sl}", f"gate{sl}"
        uvk = io["uvb_keys"]
        for s in range(128):
            k = s % NG
            kd = s % ND
            P.op("pool", lambda e, s=s, k=k: e.indirect_dma_start(
                out=gb[k], out_offset=None, in_=io["uvb"],
                in_offset=bass.IndirectOffsetOnAxis(ap=EIDX[:, s:s + 1], axis=0)),
                r=[ke] + uvk, w=[f"gb{k}"], chan=f"gg{k}")
            P.op("dve", lambda e, s=s, k=k: e.scalar_tensor_tensor(
                out=junkb, in0=gb[k][:, 0:D], scalar=1.0, in1=H2F, op0=ALU.mult, op1=ALU.mult,
                accum_out=hid[:, s:s + 1]), r=[f"gb{k}", kh], w=[f"hid{s % 8}"])
            P.op("act", lambda e, s=s: e.activation(out=gel[:, s:s + 1], in_=hid[:, s:s + 1], func=AF.Gelu_apprx_tanh),
                 r=[f"hid{s % 8}"], w=[f"gel{s % 8}"])
            P.op("act", lambda e, s=s: e.activation(out=wgt[:, s:s + 1], in_=gel[:, s:s + 1], func=AF.Copy,
                                                    scale=GATE.rearrange("p h k -> p (h k)")[:, s:s + 1]),
                 r=[f"gel{s % 8}", kg], w=[f"wgt{s % 8}"])
            P.op("act", lambda e, s=s, kd=kd: e.activation(out=dg[kd], in_=C.identf, func=AF.Copy,
                                                           scale=wgt[:, s:s + 1]),
                 r=["identf", f"wgt{s % 8}"], w=[f"dg{kd}"])
            for hf in range(2):
                P.op("pe", lambda e, s=s, k=k, kd=kd, hf=hf: e.matmul(
                    out=banks[hf], lhsT=dg[kd], rhs=gb[k][:, D + 512 * hf:D + 512 * (hf + 1)],
                    start=(s == 0), stop=(s == 127)), r=[f"dg{kd}", f"gb{k}"], w=[f"bank{hf}"])
        for hf in range(2):
            P.op("dve", lambda e, hf=hf: e.tensor_tensor(out=ot[:, 512 * hf:512 * (hf + 1)],
                                                         in0=X1[:, 512 * hf:512 * (hf + 1)], in1=banks[hf], op=ALU.add),
                 r=[kx, f"bank{hf}"], w=["ot"])
        if last_layer:
            P.op("act", lambda e: e.activation(out=junkb, in_=ot, func=AF.Square, accum_out=ss[:, 1:2]),
                 r=["ot"], w=["ss1"])
            P.op("act", lambda e: e.activation(out=ss[:, 1:2], in_=ss[:, 1:2], func=AF.Sqrt, scale=1.0 / D, bias=EPS),
                 r=["ss1"], w=["ss1"])
            P.op("dve", lambda e: e.reciprocal(out=ss[:, 1:2], in_=ss[:, 1:2]), r=["ss1"], w=["ss1"])
            P.op("dve", lambda e: e.scalar_tensor_tensor(out=ot, in0=ot, scalar=ss[:, 1:2], in1=gfb,
                                                          op0=ALU.mult, op1=ALU.mult), r=["ot", "ss1", "gfb"], w=["ot"])
        P.dma("sp", ov[i], ot, r=["ot"], w=["o_x"], chan="po", final=True)

    stage_a1(0)
    if NT > 1:
        stage_a1(1)
    stage_a2(0)
    for i in range(NT):
        if i + 2 < NT:
            stage_a1(i + 2)
        if i + 1 < NT:
            stage_a2(i + 1)
        stage_b(i)
    A.release()


LAYER_W = {
    "w_in": ([D, N_IN], F32), "g1": ([1, D], F32), "bfg": ([8, 1], F32),
    "cw": ([256, 31], F32), "cpar": ([256, 3], F32), "rw": ([256, 4], F32), "rpar": ([256, 4], F32),
    "wr": ([4, 64, 64], F32), "wi": ([4, 64, 64], F32), "w_out": ([D, D], F32), "g2": ([1, D], F32),
    "wq": ([D, 2048], F32), "kk": ([128, 16, 128], F32), "puv": ([16384, 2 * D], F32),
}


def build_fused(T):
    nc = bass.Bass("TRN2", target_bir_lowering=False)
    P = Prog(nc)
    xin = _dram(nc, "x2", [2 * T, D], F32, "ExternalInput")
    flg = _dram(nc, "flags", [128, 2], F32, "ExternalInput")
    gf = _dram(nc, "gf", [1, D], F32, "ExternalInput")
    W = [{k: _dram(nc, f"{k}_l{l}", shp, dt, "ExternalInput") for k, (shp, dt) in LAYER_W.items()} for l in range(2)]
    out = _dram(nc, "out", [T, D], F32, "ExternalOutput")
    scr = [{k: _dram(nc, f"s{sl}_{k}", shp, dt, "Internal") for k, (shp, dt) in P1_OUT(T).items()} for sl in range(2)]
    xmid = [_dram(nc, f"xmid{sl}", [T, D], F32, "Internal") for sl in range(2)]
    x1s = _dram(nc, "x1s", [T, D], F32, "Internal")
    uvb = [_dram(nc, f"uvb{l}", [16384, 2 * D], BF16, "Internal") for l in range(2)]
    NCV = 8
    RCV = 16384 // NCV

    def convert(l):
        for j in range(NCV):
            P.dma("pool", uvb[l][RCV * j:RCV * (j + 1), :], W[l]["puv"][RCV * j:RCV * (j + 1), :],
                  w=[f"uvb{l}_{j}"], chan=f"cv{j % 4}")
    A = Arena(P, 206 * 1024)
    banks = [P.ps(f"bank{i}", [128, 512], F32)[:, :] for i in range(8)]
    C = setup_common(P, A, banks)
    flags1 = A.alloc([2], F32)
    flags0 = A.alloc([2], F32)
    P.dma("sp", flags1, flg, w=["flags"], chan="f0")
    P.op("pool", lambda e: e.memset(flags0[:, 0:1], MASKV), w=["flags"])
    P.op("pool", lambda e: e.memset(flags0[:, 1:2], 0.0), w=["flags"])

    def p1(l, sl, xsrc):
        io = dict(scr[sl])
        io.update(x=xsrc, w_in=W[l]["w_in"], g1=W[l]["g1"], bfg=W[l]["bfg"])
        P.barrier()
        phase1(P, A, banks, C, T, io)

    def p2(l, sl, xsrc, xdst, last):
        prev = scr[0]
        own = scr[sl]
        io = dict(W[l])
        io.update(qT=own["qT"], cq3=own["cq3"], ggT=own["ggT"], kT_prev=prev["kT"], kT_own=own["kT"],
                  v_prev=prev["v"], v_own=own["v"], kb_prev=prev["kb_next"], kb_own=own["kb_own"],
                  y_prev=prev["yT"], y_own=own["yT"], xr_prev=prev["xrT"], xr_own=own["xrT"],
                  x=xsrc, xo=xdst, gf=gf, x1=x1s, uvb=uvb[l], uvb_keys=[f"uvb{l}_{j}" for j in range(NCV)])
        fl = flags0 if sl == 0 else flags1
        P.barrier()
        A.mark()
        mixT = A.alloc([8, T], BF16)
        phase_rnn(P, A, banks, C, T, io, mixT, fl, has_prev=(sl == 1))
        P.barrier()
        phase_conv(P, A, banks, C, T, io, mixT, fl)
        P.barrier()
        phase_attn(P, A, banks, C, T, io, mixT, fl, has_prev=(sl == 1))
        P.barrier()
        phase_wout(P, A, banks, C, T, io, mixT)
        A.release()
        P.barrier()
        phase_peer(P, A, banks, C, T, io, last)

    x0, x1 = xin[0:T, :], xin[T:2 * T, :]
    p1(0, 0, x0)
    convert(0)
    p1(0, 1, x1)
    convert(1)
    p2(0, 0, x0, xmid[0], False)
    p2(0, 1, x1, xmid[1], False)
    p1(1, 0, xmid[0])
    p1(1, 1, xmid[1])
    p2(1, 1, xmid[1], out, True)
    P.emit()
    return nc, P


def layer_weights(inp, l):
    k1 = inp["peer_k1"][l]
    k2 = inp["peer_k2"][l]
    kk = np.empty((128, 16, 128), np.float32)
    for h in range(8):
        kk[:, 2 * h, :] = k1[h].T
        kk[:, 2 * h + 1, :] = k2[h].T
    c = np.ascontiguousarray
    return {
        "w_in": inp["w_in"][l], "g1": inp["norm1_g"][l][None, :], "bfg": c(inp["b_forget"][l][:, None]),
        "cw": c(inp["conv_dw_w"][l].T),
        "cpar": c(np.stack([inp["conv_dw_b"][l], inp["conv_ln_g"][l], inp["conv_ln_b"][l]], 1)),
        "rw": c(inp["rg_conv_w"][l].T),
        "rpar": c(np.stack([inp["rg_conv_b"][l], inp["rg_b_r"][l], inp["rg_b_i"][l], inp["rg_lambda"][l]], 1)),
        "wr": inp["rg_w_r"][l], "wi": inp["rg_w_i"][l], "w_out": inp["w_out"][l], "g2": inp["norm2_g"][l][None, :],
        "wq": inp["peer_wq"][l], "kk": kk, "puv": np.concatenate([inp["peer_u"][l], inp["peer_v"][l]], axis=1),
    }


def make_in_maps(inp, T, ncore, seq_of_core):
    shared = {"gf": np.ascontiguousarray(inp["final_g"][None, :])}
    for l in range(2):
        for k, v in layer_weights(inp, l).items():
            shared[f"{k}_l{l}"] = np.ascontiguousarray(v.astype(np.float32))
    maps = []
    for c in range(ncore):
        b, half = seq_of_core(c)
        m = dict(shared)
        fl = np.zeros((128, 2), np.float32)
        if half == 0:
            fl[:, 0] = MASKV
            fl[:, 1] = 0.0
            m["x2"] = np.ascontiguousarray(np.concatenate([inp["x"][b, 0:T], inp["x"][b, 0:T]], 0))
        else:
            fl[:, 1] = 1.0
            m["x2"] = np.ascontiguousarray(inp["x"][b, 0:2 * T])
        m["flags"] = fl
        maps.append(m)
    return maps


T_CORE = 4096


def kernel(**inputs):
    inp = {k: np.asarray(v) for k, v in inputs.items()}
    T = T_CORE
    cores = list(range(8))
    nc, _ = build_fused(T)
    maps = make_in_maps(inp, T, 8, lambda c: (c // 2, c % 2))
    res = run_bass_kernel_spmd(nc, maps, core_ids=cores).results
    xs = [np.asarray(res[c]["out"]) for c in cores]
    out = np.stack([np.concatenate([xs[2 * b], xs[2 * b + 1]], 0) for b in range(4)], 0)
    return out.astype(np.float32)
```

```python
import contextlib
import numpy as np
import ml_dtypes
import concourse.bass as bass
import concourse.mybir as mybir
from concourse.bass_utils import run_bass_kernel_spmd

F32 = mybir.dt.float32
BF16 = mybir.dt.bfloat16
U32 = mybir.dt.uint32
I32 = mybir.dt.int32
U8 = mybir.dt.uint8
ALU = mybir.AluOpType
AF = mybir.ActivationFunctionType
AX = mybir.AxisListType
DSZ = {F32: 4, BF16: 2, U32: 4, I32: 4, U8: 1}

D = 1024
N_IN = 2568
Q0, K0, V0, FG0, C0, R0, G0 = 0, 512, 1024, 1536, 1544, 2056, 2312
EPS = 1e-6
SEM_LIMIT = 30000
MASKV = -30000.0


class _Cnt:
    def __init__(self, prog, name):
        self.prog, self.name = prog, name
        self.sem, self.val, self.last, self.n = None, 0, None, 0

    def bump(self, inc):
        if self.sem is None or self.val + inc > SEM_LIMIT:
            self.sem = self.prog._new_sem(f"{self.name}_{self.n}")
            self.n += 1
            self.val = 0
        self.val += inc
        self.last = (self.sem, self.val)
        return self.last


class Prog:
    ENG = ("pe", "act", "dve", "pool", "sp")

    def __init__(self, nc):
        self.nc = nc
        self.stack = contextlib.ExitStack()
        self.ops = {e: [] for e in self.ENG}
        self.cnt = {e: _Cnt(self, "e" + e) for e in self.ENG}
        self.chan = {}
        self.last_w = {}
        self.readers = {}
        self.seen = {e: {} for e in self.ENG}
        self.semown = {}
        self.out_tokens = []
        self.bar = []
        self.nops = 0

    def _new_sem(self, name):
        return self.stack.enter_context(self.nc.semaphore(name))

    def sb(self, name, shape, dtype):
        return self.stack.enter_context(self.nc.sbuf_tensor(name, list(shape), dtype))

    def ps(self, name, shape, dtype):
        return self.stack.enter_context(self.nc.psum_tensor(name, list(shape), dtype))

    def barrier(self):
        toks = [c.last for c in self.cnt.values() if c.last is not None]
        toks += [c.last for n, c in self.chan.items() if c.last is not None and not str(n).startswith("cv")]
        self.bar = toks

    def op(self, eng, fn, r=(), w=(), chan=None, final=False):
        self.nops += 1
        deps = list(self.bar)
        for k in r:
            t = self.last_w.get(k)
            if t is not None:
                deps.append(t)
        for k in w:
            t = self.last_w.get(k)
            if t is not None:
                deps.append(t)
            deps.extend(self.readers.get(k, ()))
        if chan is not None:
            c = self.chan.get(chan)
            if c is None:
                c = self.chan[chan] = _Cnt(self, "c" + str(chan))
            if c.last is not None:
                deps.append(c.last)
            tok = c.bump(16)
            inc = 16
        else:
            tok = self.cnt[eng].bump(1)
            self.semown[id(tok[0])] = eng
            inc = 1
        waits = {}
        for (s, v) in deps:
            if eng == "pe" and chan is None and self.semown.get(id(s)) == "pe":
                continue
            sid = id(s)
            if self.seen[eng].get(sid, 0) >= v:
                continue
            if sid not in waits or waits[sid][1] < v:
                waits[sid] = (s, v)
        for sid, (s, v) in waits.items():
            self.seen[eng][sid] = v
        self.ops[eng].append((list(waits.values()), fn, tok[0], inc))
        for k in r:
            if k not in w:
                self.readers.setdefault(k, []).append(tok)
        for k in w:
            self.last_w[k] = tok
            self.readers[k] = []
        if final:
            self.out_tokens.append(tok)
        return tok

    def dma(self, eng, out, in_, r=(), w=(), chan=None, final=False, **kw):
        return self.op(eng, lambda e: e.dma_start(out=out, in_=in_, **kw), r=r, w=w,
                       chan=chan, final=final)

    def emit(self):
        nc = self.nc
        fin = {}
        for (s, v) in self.out_tokens:
            if id(s) not in fin or fin[id(s)][1] < v:
                fin[id(s)] = (s, v)
        with nc.Block() as block:
            def run(engname):
                def body(e):
                    for (waits, fn, sem, inc) in self.ops[engname]:
                        for (s, v) in waits:
                            e.wait_ge(s, v)
                        fn(e).then_inc(sem, inc)
                    if engname == "sp":
                        for (s, v) in fin.values():
                            e.wait_ge(s, v)
                return body
            block.tensor(run("pe"))
            block.scalar(run("act"))
            block.vector(run("dve"))
            block.gpsimd(run("pool"))
            block.sync(run("sp"))
        self.stack.close()


class Arena:
    def __init__(self, P, nbytes):
        self.t = P.sb("arena", [128, nbytes], U8)
        self.nbytes = nbytes
        self.off = 0
        self.marks = []

    def mark(self):
        self.marks.append(self.off)

    def release(self):
        self.off = self.marks.pop()

    def alloc(self, free_shape, dtype):
        n = int(np.prod(free_shape)) * DSZ[dtype]
        n_al = (n + 63) // 64 * 64
        assert self.off + n_al <= self.nbytes, (self.off, n_al, self.nbytes)
        ap = self.t[:, self.off:self.off + n].bitcast(dtype)
        self.off += n_al
        if len(free_shape) == 2:
            ap = ap.rearrange("p (a b) -> p a b", b=free_shape[1])
        elif len(free_shape) == 3:
            ap = ap.rearrange("p (a b c) -> p a b c", b=free_shape[1], c=free_shape[2])
        return ap


def _dram(nc, name, shape, dtype, kind):
    return nc.dram_tensor(name, list(shape), dtype, kind=kind).ap()


class Ctx:
    pass


def setup_common(P, A, banks):
    C = Ctx()
    C.identf = A.alloc([128], F32)
    C.identb = A.alloc([128], BF16)
    C.onesf = A.alloc([128], F32)
    C.maskT = A.alloc([128], BF16)
    mf = A.alloc([128], F32)
    P.op("pool", lambda e: e.memset(C.identf, 0.0), w=["identf"])
    P.op("pool", lambda e: e.affine_select(out=C.identf, in_=C.identf, pattern=[[-1, 128]],
                                           compare_op=ALU.not_equal, fill=1.0, base=0,
                                           channel_multiplier=1), r=["identf"], w=["identf"])
    P.op("dve", lambda e: e.tensor_copy(out=C.identb, in_=C.identf), r=["identf"], w=["identb"])
    P.op("pool", lambda e: e.memset(C.onesf, 1.0), w=["onesf"])
    P.op("pool", lambda e: e.memset(mf, 0.0), w=["mf"])
    P.op("pool", lambda e: e.affine_select(out=mf, in_=mf, pattern=[[1, 128]],
                                           compare_op=ALU.is_ge, fill=MASKV, base=0,
                                           channel_multiplier=-1), r=["mf"], w=["mf"])
    P.op("dve", lambda e: e.tensor_copy(out=C.maskT, in_=mf), r=["mf"], w=["maskT"])
    return C


def phase1(P, A, banks, C, T, io):
    NB = T // 128
    NCH = T // 512
    A.mark()
    w_sb = A.alloc([8, N_IN], BF16)
    g_bc = A.alloc([D], F32)
    bfg = A.alloc([1], F32)
    nbfg = A.alloc([1], F32)
    xt = [A.alloc([D], F32) for _ in range(4)]
    sqj = A.alloc([D], F32)
    ss = [A.alloc([1], F32) for _ in range(2)]
    hb = [A.alloc([D], BF16) for _ in range(2)]
    hT = [A.alloc([8, 512], BF16) for _ in range(2)]
    stg = [A.alloc([512], F32) for _ in range(4)]
    stb = [A.alloc([512], BF16) for _ in range(4)]
    sg = [A.alloc([512], F32) for _ in range(2)]
    lf = A.alloc([T], F32)
    cum = A.alloc([T], F32)
    cq3 = A.alloc([3, T], BF16)
    kbs = A.alloc([NB, 8], F32)
    onecol = A.alloc([1], F32)

    wv = io["w_in"].rearrange("(kc p) n -> p kc n", p=128)
    for kc in range(8):
        for hf in range(2):
            cs = slice(1284 * hf, 1284 * (hf + 1))
            P.dma("pool", w_sb[:, kc, cs], wv[:, kc, cs], w=["w_sb"], chan=f"w1_{(2 * kc + hf) % 4}")
    P.dma("sp", g_bc, io["g1"].to_broadcast([128, D]), w=["g_bc"], chan="c0")
    P.dma("sp", bfg[0:8, :], io["bfg"], w=["bfg"], chan="c1")
    P.op("dve", lambda e: e.tensor_scalar(out=nbfg[0:8, :], in0=bfg[0:8, :], scalar1=-1.0, scalar2=None,
                                          op0=ALU.mult), r=["bfg"], w=["nbfg"])
    P.op("pool", lambda e: e.memset(onecol, 1.0), w=["onecol"])

    fm = []
    for j in range(4):
        fm.append((Q0 + 128 * j, 128, "q", 128 * j))
    for j in range(4):
        fm.append((K0 + 128 * j, 128, "k", 128 * j))
    for j in range(2):
        fm.append((C0 + 256 + 128 * j, 128, "cg", 128 * j))
        fm.append((C0 + 128 * j, 128, "ca", 128 * j))
    for j in range(2):
        fm.append((R0 + 128 * j, 128, "r", 128 * j))
    for j in range(2):
        fm.append((G0 + 128 * j, 128, "g", 128 * j))
    fm.append((FG0, 8, "f", 0))

    xv = io["x"].rearrange("(n p) d -> n p d", p=128)
    cnt = {"pz": 0, "stg": 0, "stb": 0, "sg": 0}
    for c in range(NCH):
        hs = c % 2
        for tt in range(4):
            i = 4 * c + tt
            xs = i % 4
            s2 = i % 2
            P.dma("sp", xt[xs], xv[i], w=[f"xt{xs}"], chan=f"x{xs}")
            P.op("act", lambda e, xs=xs, s2=s2: e.activation(out=sqj, in_=xt[xs], func=AF.Square,
                                                              accum_out=ss[s2]),
                 r=[f"xt{xs}"], w=["sqj", f"ss{s2}"])
            P.op("act", lambda e, s2=s2: e.activation(out=ss[s2], in_=ss[s2], func=AF.Sqrt,
                                                      scale=1.0 / D, bias=EPS),
                 r=[f"ss{s2}"], w=[f"ss{s2}"])
            P.op("dve", lambda e, s2=s2: e.reciprocal(out=ss[s2], in_=ss[s2]), r=[f"ss{s2}"], w=[f"ss{s2}"])
            P.op("dve", lambda e, xs=xs, s2=s2: e.scalar_tensor_tensor(
                out=hb[s2], in0=xt[xs], scalar=ss[s2], in1=g_bc, op0=ALU.mult, op1=ALU.mult),
                r=[f"xt{xs}", f"ss{s2}", "g_bc"], w=[f"hb{s2}"])
            tb = 6 + s2
            tpv = banks[tb].bitcast(BF16).rearrange("p (a b) -> p a b", b=128)
            for kc in range(8):
                P.op("pe", lambda e, kc=kc, s2=s2, tpv=tpv: e.transpose(
                    out=tpv[:, kc, :], in_=hb[s2][:, 128 * kc:128 * (kc + 1)], identity=C.identb),
                    r=[f"hb{s2}", "identb"], w=[f"bank{tb}"])
            P.op("act" if tt % 2 == 0 else "dve",
                 (lambda e, tpv=tpv, hs=hs, tt=tt: e.copy(out=hT[hs][:, :, 128 * tt:128 * (tt + 1)], in_=tpv))
                 if tt % 2 == 0 else
                 (lambda e, tpv=tpv, hs=hs, tt=tt: e.tensor_copy(out=hT[hs][:, :, 128 * tt:128 * (tt + 1)], in_=tpv)),
                 r=[f"bank{tb}"], w=[f"hT{hs}"])
        tsl = slice(512 * c, 512 * (c + 1))
        for (col0, wd, kind, row0) in fm:
            pb = cnt["pz"] % 4
            cnt["pz"] += 1
            pz = banks[pb]
            for kc in range(8):
                P.op("pe", lambda e, kc=kc, pz=pz, col0=col0, wd=wd, hs=hs: e.matmul(
                    out=pz[0:wd, :], lhsT=w_sb[:, kc, col0:col0 + wd], rhs=hT[hs][:, kc, :],
                    start=(kc == 0), stop=(kc == 7)),
                    r=["w_sb", f"hT{hs}"], w=[f"bank{pb}"])
            if kind in ("q", "k"):
                sb_i = cnt["stb"] % 4
                cnt["stb"] += 1
                eng = "act" if (sb_i % 2 == 0) else "dve"
                if eng == "act":
                    P.op("act", lambda e, pz=pz, sb_i=sb_i: e.copy(out=stb[sb_i], in_=pz),
                         r=[f"bank{pb}"], w=[f"stb{sb_i}"])
                else:
                    P.op("dve", lambda e, pz=pz, sb_i=sb_i: e.tensor_copy(out=stb[sb_i], in_=pz),
                         r=[f"bank{pb}"], w=[f"stb{sb_i}"])
                dst = io["qT" if kind == "q" else "kT"][row0:row0 + 128, tsl]
                P.dma("pool", dst, stb[sb_i], r=[f"stb{sb_i}"], w=["o_" + kind], chan=f"ob{sb_i}", final=True)
            elif kind == "cg":
                gi = cnt["sg"] % 2
                cnt["sg"] += 1
                P.op("act", lambda e, pz=pz, gi=gi: e.activation(out=sg[gi], in_=pz, func=AF.Sigmoid),
                     r=[f"bank{pb}"], w=[f"sg{gi}"])
                last_sg = gi
            elif kind == "ca":
                si = cnt["stg"] % 4
                cnt["stg"] += 1
                gi = last_sg
                P.op("dve", lambda e, pz=pz, si=si, gi=gi: e.tensor_tensor(out=stg[si], in0=pz, in1=sg[gi],
                                                                           op=ALU.mult),
                     r=[f"bank{pb}", f"sg{gi}"], w=[f"stg{si}"])
                P.dma("pool", io["yT"][row0:row0 + 128, tsl], stg[si], r=[f"stg{si}"], w=["o_y"],
                      chan=f"of{si}", final=True)
            elif kind in ("r", "g"):
                si = cnt["stg"] % 4
                cnt["stg"] += 1
                if kind == "r":
                    P.op("act", lambda e, pz=pz, si=si: e.copy(out=stg[si], in_=pz),
                         r=[f"bank{pb}"], w=[f"stg{si}"])
                else:
                    P.op("act", lambda e, pz=pz, si=si: e.activation(out=stg[si], in_=pz,
                                                                     func=AF.Gelu_apprx_tanh),
                         r=[f"bank{pb}"], w=[f"stg{si}"])
                dst = io["xrT" if kind == "r" else "ggT"][row0:row0 + 128, tsl]
                P.dma("pool", dst, stg[si], r=[f"stg{si}"], w=["o_" + kind], chan=f"of{si}", final=True)
            else:
                si = cnt["stg"] % 4
                cnt["stg"] += 1
                P.op("act", lambda e, pz=pz, si=si: e.activation(out=stg[si][0:8, :], in_=pz[0:8, :], func=AF.Exp,
                                                                 scale=-1.0, bias=nbfg[0:8, :]),
                     r=[f"bank{pb}", "nbfg"], w=[f"stg{si}"])
                P.op("act", lambda e, si=si, tsl=tsl: e.activation(out=lf[0:8, tsl], in_=stg[si][0:8, :],
                                                                   func=AF.Ln, scale=1.0, bias=1.0),
                     r=[f"stg{si}"], w=["lf"])
        for tt in range(4):
            pb = 4 + (tt % 2)
            pv = banks[pb]
            for kc in range(8):
                P.op("pe", lambda e, kc=kc, pv=pv, hs=hs, tt=tt: e.matmul(
                    out=pv, lhsT=hT[hs][:, kc, 128 * tt:128 * (tt + 1)], rhs=w_sb[:, kc, V0:V0 + 512],
                    start=(kc == 0), stop=(kc == 7)),
                    r=["w_sb", f"hT{hs}"], w=[f"bank{pb}"])
            sb_i = cnt["stb"] % 4
            cnt["stb"] += 1
            P.op("dve", lambda e, pv=pv, sb_i=sb_i: e.tensor_copy(out=stb[sb_i], in_=pv),
                 r=[f"bank{pb}"], w=[f"stb{sb_i}"])
            i = 4 * c + tt
            P.dma("pool", io["v"][128 * i:128 * (i + 1), :], stb[sb_i], r=[f"stb{sb_i}"], w=["o_v"],
                  chan=f"ob{sb_i}", final=True)

    P.op("dve", lambda e: e.tensor_tensor_scan(out=cum[0:8, :], data0=onecol[0:8, 0:1].to_broadcast([8, T]),
                                               data1=lf[0:8, :], initial=0.0, op0=ALU.mult, op1=ALU.subtract),
         r=["lf", "onecol"], w=["cum"])
    P.op("dve", lambda e: e.tensor_scalar(out=lf[0:8, :], in0=cum[0:8, :], scalar1=8.0, scalar2=None, op0=ALU.mult),
         r=["cum"], w=["lf"])
    for j in range(3):
        P.op("dve", lambda e, j=j: e.tensor_copy(out=cq3[0:8, j, :], in_=lf[0:8, :]), r=["lf"], w=["cq3"])
        if j < 2:
            P.op("dve", lambda e, j=j: e.tensor_tensor(out=lf[0:8, :], in0=lf[0:8, :], in1=cq3[0:8, j, :],
                                                       op=ALU.subtract), r=["lf", "cq3"], w=["lf"])
    P.dma("sp", io["cq3"], cq3[0:8, :, :], r=["cq3"], w=["o_cq3"], chan="c2", final=True)
    for which in range(2):
        pb = 4 + which
        tpc = banks[pb][:, 0:NB * 8].rearrange("p (a b) -> p a b", b=8)
        if which == 0:
            src = cum
            rk = "cum"
        else:
            P.op("dve", lambda e: e.tensor_scalar(out=lf[0:8, :], in0=cum[0:8, :], scalar1=cum[0:8, T - 1:T],
                                                  scalar2=None, op0=ALU.subtract), r=["cum"], w=["lf"])
            src = lf
            rk = "lf"
        for b in range(NB):
            P.op("pe", lambda e, b=b, tpc=tpc, src=src: e.transpose(
                out=tpc[:, b, :], in_=src[0:8, 128 * b:128 * (b + 1)], identity=C.identf[0:8, 0:8]),
                r=[rk, "identf"], w=[f"bank{pb}"])
        P.op("act", lambda e, tpc=tpc: e.activation(out=kbs, in_=tpc, func=AF.Copy, scale=-1.0),
             r=[f"bank{pb}"], w=["kbs"])
        P.dma("sp", io["kb_own" if which == 0 else "kb_next"], kbs, r=["kbs"], w=[f"o_kb{which}"],
              chan="c3", final=True)
    A.release()


P1_OUT = lambda T: {
    "qT": ([512, T], BF16), "kT": ([512, T], BF16), "v": ([T, 512], BF16), "cq3": ([8, 3, T], BF16),
    "kb_own": ([128, T // 128, 8], F32), "kb_next": ([128, T // 128, 8], F32),
    "yT": ([256, T], F32), "xrT": ([256, T], F32), "ggT": ([256, T], F32),
}


def build_k1(T):
    nc = bass.Bass("TRN2", target_bir_lowering=False)
    P = Prog(nc)
    io = {
        "x": _dram(nc, "x", [T, D], F32, "ExternalInput"),
        "w_in": _dram(nc, "w_in", [D, N_IN], F32, "ExternalInput"),
        "g1": _dram(nc, "g1", [1, D], F32, "ExternalInput"),
        "bfg": _dram(nc, "bfg", [8, 1], F32, "ExternalInput"),
    }
    for k, (shp, dt) in P1_OUT(T).items():
        io[k] = _dram(nc, k, shp, dt, "ExternalOutput")
    A = Arena(P, 190 * 1024)
    banks = [P.ps(f"bank{i}", [128, 512], F32)[:, :] for i in range(8)]
    C = setup_common(P, A, banks)
    phase1(P, A, banks, C, T, io)
    P.emit()
    return nc


def phase_rnn(P, A, banks, C, T, io, mixT, flags, has_prev=True):
    A.mark()
    SEG = min(1024, T)
    nseg = 2 * T // SEG
    xe = A.alloc([3 + SEG], F32)
    xc = A.alloc([SEG], F32)
    xcb = A.alloc([SEG], BF16)
    rr = A.alloc([SEG], F32)
    ii = A.alloc([SEG], F32)
    aa = A.alloc([SEG], F32)
    mm = A.alloc([SEG], F32)
    hsb = A.alloc([SEG], F32)
    ggb = A.alloc([SEG], F32)
    rwt = A.alloc([4], F32)
    rpt = A.alloc([4], F32)
    cneg = A.alloc([2], F32)
    carry = A.alloc([1], F32)
    wbf = A.alloc([128], F32)
    wbb = [A.alloc([128], BF16) for _ in range(2)]
    for cc in range(2):
        rows = slice(128 * cc, 128 * (cc + 1))
        P.dma("sp", rwt, io["rw"][rows, :], w=["rwt"], chan="r0")
        P.dma("sp", rpt, io["rpar"][rows, :], w=["rpt"], chan="r1")
        P.op("act", lambda e: e.activation(out=cneg[:, 0:1], in_=rpt[:, 3:4], func=AF.Exp, scale=-1.0),
             r=["rpt"], w=["cneg"])
        P.op("act", lambda e: e.activation(out=cneg[:, 0:1], in_=cneg[:, 0:1], func=AF.Ln, scale=1.0, bias=1.0),
             r=["cneg"], w=["cneg"])
        P.op("dve", lambda e: e.tensor_scalar(out=cneg[:, 1:2], in0=cneg[:, 0:1], scalar1=-16.0, scalar2=None,
                                              op0=ALU.mult), r=["cneg"], w=["cneg"])
        P.op("dve", lambda e: e.tensor_scalar(out=cneg[:, 0:1], in0=cneg[:, 0:1], scalar1=-8.0, scalar2=None,
                                              op0=ALU.mult), r=["cneg"], w=["cneg"])
        for wi, nm in enumerate(("wr", "wi")):
            P.op("pool", lambda e: e.memset(wbf, 0.0), w=["wbf"])
            P.dma("sp", wbf[0:64, 0:64], io[nm][2 * cc], w=["wbf"], chan="r2")
            P.dma("sp", wbf[64:128, 64:128], io[nm][2 * cc + 1], w=["wbf"], chan="r3")
            P.op("dve", lambda e, wi=wi: e.tensor_copy(out=wbb[wi], in_=wbf), r=["wbf"], w=[f"wbb{wi}"])
        P.op("pool", lambda e: e.memset(carry, 0.0), w=["carry"])
        for s in range(nseg):
            t0 = s * SEG
            if not has_prev and t0 < T:
                continue
            if not has_prev and t0 == T:
                P.op("pool", lambda e: e.memset(xe[:, 0:3], 0.0), w=["xe"])
                P.dma("sp", xe[:, 3:3 + SEG], io["xr_own"][rows, 0:SEG], w=["xe"], chan="r4")
            elif s == 0:
                P.op("pool", lambda e: e.memset(xe[:, 0:3], 0.0), w=["xe"])
                P.dma("sp", xe[:, 3:3 + SEG], io["xr_prev"][rows, 0:SEG], w=["xe"], chan="r4")
            elif t0 < T:
                P.dma("sp", xe, io["xr_prev"][rows, t0 - 3:t0 + SEG], w=["xe"], chan="r4")
            elif t0 == T:
                P.dma("sp", xe[:, 0:3], io["xr_prev"][rows, T - 3:T], w=["xe"], chan="r4")
                P.dma("sp", xe[:, 3:3 + SEG], io["xr_own"][rows, 0:SEG], w=["xe"], chan="r4")
            else:
                P.dma("sp", xe, io["xr_own"][rows, t0 - T - 3:t0 - T + SEG], w=["xe"], chan="r4")
            if t0 == T:
                P.op("dve", lambda e: e.tensor_scalar(out=xe[:, 0:3], in0=xe[:, 0:3], scalar1=flags[:, 1:2],
                                                      scalar2=None, op0=ALU.mult), r=["xe", "flags"], w=["xe"])
            P.op("dve", lambda e: e.tensor_scalar(out=xc, in0=xe[:, 3:3 + SEG], scalar1=rwt[:, 3:4],
                                                  scalar2=rpt[:, 0:1], op0=ALU.mult, op1=ALU.add),
                 r=["xe", "rwt", "rpt"], w=["xc"])
            for k in range(3):
                P.op("dve", lambda e, k=k: e.scalar_tensor_tensor(out=xc, in0=xe[:, k:k + SEG], scalar=rwt[:, k:k + 1],
                                                                   in1=xc, op0=ALU.mult, op1=ALU.add),
                     r=["xe", "rwt", "xc"], w=["xc"])
            P.op("act", lambda e: e.copy(out=xcb, in_=xc), r=["xc"], w=["xcb"])
            for ch in range(SEG // 512):
                sl = slice(512 * ch, 512 * (ch + 1))
                for wi, (dst, bcol) in enumerate(((rr, 1), (ii, 2))):
                    pb = 2 * (ch % 2) + wi
                    P.op("pe", lambda e, wi=wi, pb=pb, sl=sl: e.matmul(out=banks[pb], lhsT=wbb[wi], rhs=xcb[:, sl],
                                                                        start=True, stop=True),
                         r=[f"wbb{wi}", "xcb"], w=[f"bank{pb}"])
                    P.op("act", lambda e, dst=dst, pb=pb, sl=sl, bcol=bcol: e.activation(
                        out=dst[:, sl], in_=banks[pb], func=AF.Sigmoid, bias=rpt[:, bcol:bcol + 1], scale=1.0),
                        r=[f"bank{pb}", "rpt"], w=["rr" if wi == 0 else "ii"])
            P.op("act", lambda e: e.activation(out=aa, in_=rr, func=AF.Exp, scale=cneg[:, 0:1]),
                 r=["rr", "cneg"], w=["aa"])
            P.op("act", lambda e: e.activation(out=mm, in_=rr, func=AF.Exp, scale=cneg[:, 1:2]),
                 r=["rr", "cneg"], w=["mm"])
            P.op("act", lambda e: e.activation(out=mm, in_=mm, func=AF.Sqrt, scale=-1.0, bias=1.0),
                 r=["mm"], w=["mm"])
            P.op("dve", lambda e: e.tensor_tensor(out=ii, in0=ii, in1=mm, op=ALU.mult), r=["ii", "mm"], w=["ii"])
            P.op("dve", lambda e: e.tensor_tensor(out=ii, in0=ii, in1=xc, op=ALU.mult), r=["ii", "xc"], w=["ii"])
            if t0 == T:
                P.op("dve", lambda e: e.tensor_scalar(out=aa[:, 0:1], in0=aa[:, 0:1], scalar1=flags[:, 1:2],
                                                      scalar2=None, op0=ALU.mult), r=["aa", "flags"], w=["aa"])
            P.op("dve", lambda e: e.tensor_tensor_scan(out=hsb, data0=aa, data1=ii, initial=carry[:, 0:1],
                                                       op0=ALU.mult, op1=ALU.add),
                 r=["aa", "ii", "carry"], w=["hsb"])
            P.op("dve", lambda e: e.tensor_copy(out=carry, in_=hsb[:, SEG - 1:SEG]), r=["hsb"], w=["carry"])
            if t0 >= T:
                o0 = t0 - T
                P.dma("sp", ggb, io["ggT"][rows, o0:o0 + SEG], w=["ggb"], chan="r5")
                P.op("dve", lambda e, o0=o0, cc=cc: e.tensor_tensor(out=mixT[:, 6 + cc, o0:o0 + SEG], in0=hsb, in1=ggb,
                                                                     op=ALU.mult), r=["hsb", "ggb"], w=["mixT"])
    A.release()


def phase_conv(P, A, banks, C, T, io, mixT, flags):
    A.mark()
    ye = [A.alloc([30 + T], F32) for _ in range(2)]
    acc = [A.alloc([T], F32) for _ in range(2)]
    cwt = [A.alloc([31], F32) for _ in range(2)]
    cpt = [A.alloc([3], F32) for _ in range(2)]
    onesm = A.alloc([128], F32)
    sq = [A.alloc([512], F32) for _ in range(2)]
    mt = A.alloc([512], F32)
    vt = A.alloc([512], F32)
    zt = A.alloc([512], F32)
    P.op("pool", lambda e: e.memset(onesm, 1.0 / 256.0), w=["onesm"])
    for cc in range(2):
        rows = slice(128 * cc, 128 * (cc + 1))
        P.dma("sp", ye[cc][:, 0:30], io["y_prev"][rows, T - 30:T], w=[f"ye{cc}"], chan=f"v0{cc}")
        P.dma("sp", ye[cc][:, 30:30 + T], io["y_own"][rows, :], w=[f"ye{cc}"], chan=f"v0{cc}")
        P.dma("sp", cwt[cc], io["cw"][rows, :], w=[f"cwt{cc}"], chan="v1")
        P.dma("sp", cpt[cc], io["cpar"][rows, :], w=[f"cpt{cc}"], chan="v2")
        P.op("dve", lambda e, cc=cc: e.tensor_scalar(out=ye[cc][:, 0:30], in0=ye[cc][:, 0:30], scalar1=flags[:, 1:2],
                                                     scalar2=None, op0=ALU.mult), r=[f"ye{cc}", "flags"], w=[f"ye{cc}"])
        P.op("dve", lambda e, cc=cc: e.tensor_scalar(out=acc[cc], in0=ye[cc][:, 0:T], scalar1=cwt[cc][:, 0:1],
                                                     scalar2=cpt[cc][:, 0:1], op0=ALU.mult, op1=ALU.add),
             r=[f"ye{cc}", f"cwt{cc}", f"cpt{cc}"], w=[f"acc{cc}"])
        for k in range(1, 31):
            P.op("dve", lambda e, cc=cc, k=k: e.scalar_tensor_tensor(out=acc[cc], in0=ye[cc][:, k:k + T],
                                                                      scalar=cwt[cc][:, k:k + 1], in1=acc[cc],
                                                                      op0=ALU.mult, op1=ALU.add),
                 r=[f"ye{cc}", f"cwt{cc}", f"acc{cc}"], w=[f"acc{cc}"])
    for ch in range(T // 512):
        sl = slice(512 * ch, 512 * (ch + 1))
        for cc in range(2):
            P.op("act", lambda e, cc=cc, sl=sl: e.activation(out=sq[cc], in_=acc[cc][:, sl], func=AF.Square),
                 r=[f"acc{cc}"], w=[f"sq{cc}"])
        for cc in range(2):
            P.op("pe", lambda e, cc=cc, sl=sl: e.matmul(out=banks[4], lhsT=onesm, rhs=acc[cc][:, sl],
                                                        start=(cc == 0), stop=(cc == 1)),
                 r=["onesm", f"acc{cc}"], w=["bank4"])
        for cc in range(2):
            P.op("pe", lambda e, cc=cc: e.matmul(out=banks[5], lhsT=onesm, rhs=sq[cc],
                                                 start=(cc == 0), stop=(cc == 1)),
                 r=["onesm", f"sq{cc}"], w=["bank5"])
        P.op("act", lambda e: e.copy(out=mt, in_=banks[4]), r=["bank4"], w=["mt"])
        P.op("dve", lambda e: e.tensor_tensor(out=vt, in0=mt, in1=mt, op=ALU.mult), r=["mt"], w=["vt"])
        P.op("dve", lambda e: e.tensor_tensor(out=vt, in0=banks[5], in1=vt, op=ALU.subtract), r=["bank5", "vt"], w=["vt"])
        P.op("act", lambda e: e.activation(out=vt, in_=vt, func=AF.Sqrt, scale=1.0, bias=EPS), r=["vt"], w=["vt"])
        P.op("dve", lambda e: e.reciprocal(out=vt, in_=vt), r=["vt"], w=["vt"])
        for cc in range(2):
            P.op("dve", lambda e, cc=cc, sl=sl: e.tensor_tensor(out=zt, in0=acc[cc][:, sl], in1=mt, op=ALU.subtract),
                 r=[f"acc{cc}", "mt"], w=["zt"])
            P.op("dve", lambda e: e.tensor_tensor(out=zt, in0=zt, in1=vt, op=ALU.mult), r=["zt", "vt"], w=["zt"])
            P.op("act", lambda e, cc=cc, sl=sl: e.activation(out=mixT[:, 4 + cc, sl], in_=zt, func=AF.Silu,
                                                             scale=cpt[cc][:, 1:2], bias=cpt[cc][:, 2:3]),
                 r=["zt", f"cpt{cc}"], w=["mixT"])
    A.release()


def phase_attn(P, A, banks, C, T, io, mixT, flags, has_prev=True):
    A.mark()
    NB = T // 128
    NB2 = 2 * NB
    NCH = T // 512
    KA = [A.alloc([2 * T], BF16) for _ in range(2)]
    QA = [A.alloc([T], BF16) for _ in range(2)]
    VA = [A.alloc([NB2, 128], BF16) for _ in range(2)]
    kbm = A.alloc([NB2, 8], F32)
    PT = [A.alloc([512], BF16) for _ in range(3)]
    Osb = A.alloc([512], F32)
    rrow = A.alloc([512], F32)
    for s in range(2):
        P.op("pool", lambda e, s=s: e.memset(KA[s][64:67, :], 1.0), w=[f"KA{s}"])
    P.op("pool", lambda e: e.memset(VA[0][:, :, 64:128], 1.0), w=["VA0"])
    P.op("pool", lambda e: e.memset(VA[1][:, :, 0:64], 1.0), w=["VA1"])
    P.dma("sp", kbm[:, 0:NB, :], io["kb_prev"], w=["kbm"], chan="a0")
    P.dma("sp", kbm[:, NB:NB2, :], io["kb_own"], w=["kbm"], chan="a0")
    P.op("dve", lambda e: e.tensor_scalar(out=kbm[:, 0:NB, :], in0=kbm[:, 0:NB, :], scalar1=flags[:, 0:1],
                                          scalar2=None, op0=ALU.add), r=["kbm", "flags"], w=["kbm"])
    vpv = io["v_prev"].rearrange("(b p) c -> p b c", p=128)
    vov = io["v_own"].rearrange("(b p) c -> p b c", p=128)

    def load_head(h):
        s = h % 2
        if has_prev:
            P.dma("sp", KA[s][0:64, 0:T], io["kT_prev"][64 * h:64 * (h + 1), :], w=[f"KA{s}"], chan=f"ka{s}")
        P.dma("sp", KA[s][0:64, T:2 * T], io["kT_own"][64 * h:64 * (h + 1), :], w=[f"KA{s}"], chan=f"ka{s}")
        P.dma("sp", QA[s][0:64, :], io["qT"][64 * h:64 * (h + 1), :], w=[f"QA{s}"], chan=f"qa{s}")
        P.dma("sp", QA[s][64:67, :], io["cq3"][h], w=[f"QA{s}"], chan=f"qb{s}")
        c0 = 0 if s == 0 else 64
        if has_prev:
            P.dma("sp", VA[s][:, 0:NB, c0:c0 + 64], vpv[:, :, 64 * h:64 * (h + 1)], w=[f"VA{s}"], chan=f"va{s}")
        P.dma("sp", VA[s][:, NB:NB2, c0:c0 + 64], vov[:, :, 64 * h:64 * (h + 1)], w=[f"VA{s}"], chan=f"va{s}")

    tiles = []
    for h in range(8):
        for qc in range(NCH):
            blks = [(g, None) for g in range(NB)] if has_prev else []
            for ob in range(4 * qc + 4):
                blks.append((NB + ob, (ob - 4 * qc) if ob >= 4 * qc else None))
            for bi, (g, j) in enumerate(blks):
                tiles.append(dict(h=h, qc=qc, g=g, j=j, first=(bi == 0), last=(bi == len(blks) - 1),
                                  grp=h * NCH + qc, lasth=(qc == NCH - 1 and bi == len(blks) - 1)))
    SK = 2
    load_head(0)
    load_head(1)
    for i in range(len(tiles) + SK):
        if i < len(tiles):
            t = tiles[i]
            h, qc, g, j = t["h"], t["qc"], t["g"], t["j"]
            s = h % 2
            c0 = 0 if j is None else 128 * j
            zb = i % 3
            Z = banks[zb]
            P.op("pe", lambda e, Z=Z, s=s, g=g, qc=qc, c0=c0, j=j: e.matmul(
                out=Z[:, c0:512], lhsT=KA[s][0:67, 128 * g:128 * (g + 1)],
                rhs=QA[s][0:67, 512 * qc + c0:512 * (qc + 1)], start=True, stop=(j is None)),
                r=[f"KA{s}", f"QA{s}"], w=[f"bank{zb}"])
            if j is not None:
                P.op("pe", lambda e, Z=Z, c0=c0: e.matmul(out=Z[:, c0:c0 + 128], lhsT=C.identb, rhs=C.maskT,
                                                           start=False, stop=True),
                     r=["identb", "maskT"], w=[f"bank{zb}"])
            P.op("act", lambda e, Z=Z, zb=zb, c0=c0, g=g, h=h: e.activation(
                out=PT[zb][:, c0:512], in_=Z[:, c0:512], func=AF.Exp, scale=0.125, bias=kbm[:, g, h:h + 1]),
                r=[f"bank{zb}", "kbm"], w=[f"PT{zb}"])
        if i >= SK:
            t = tiles[i - SK]
            h, qc, g, j = t["h"], t["qc"], t["g"], t["j"]
            s = h % 2
            c0 = 0 if j is None else 128 * j
            zb = (i - SK) % 3
            ob = 3 + (t["grp"] % 2)
            O = banks[ob]
            P.op("pe", lambda e, O=O, s=s, g=g, zb=zb, c0=c0, t=t: e.matmul(
                out=O[:, c0:512], lhsT=VA[s][:, g, :], rhs=PT[zb][:, c0:512], start=t["first"], stop=t["last"]),
                r=[f"VA{s}", f"PT{zb}"], w=[f"bank{ob}"])
            if t["last"]:
                p0 = 64 if s == 0 else 0
                r0 = 0 if s == 0 else 64
                P.op("act", lambda e, O=O: e.copy(out=Osb, in_=O), r=[f"bank{ob}"], w=["Osb"])
                P.op("dve", lambda e, p0=p0: e.reciprocal(out=rrow[p0:p0 + 1, :], in_=Osb[p0:p0 + 1, :]),
                     r=["Osb"], w=["rrow"])
                P.op("pe", lambda e, p0=p0: e.matmul(out=banks[5], lhsT=C.onesf[p0:p0 + 1, :], rhs=rrow[p0:p0 + 1, :],
                                                     start=True, stop=True), r=["onesf", "rrow"], w=["bank5"])
                P.op("dve", lambda e, r0=r0, h=h, qc=qc: e.tensor_tensor(
                    out=mixT[r0:r0 + 64, h // 2, 512 * qc:512 * (qc + 1)], in0=Osb[r0:r0 + 64, :],
                    in1=banks[5][r0:r0 + 64, :], op=ALU.mult), r=["Osb", "bank5"], w=["mixT"])
            if t["lasth"] and h + 2 < 8:
                load_head(h + 2)
    A.release()


NG = 16
ND = 4


def phase_wout(P, A, banks, C, T, io, mixT):
    A.mark()
    NT = T // 128
    wo = A.alloc([8, D], BF16)
    xt = [A.alloc([D], F32) for _ in range(2)]
    xs1 = [A.alloc([D], F32) for _ in range(2)]
    wov = io["w_out"].rearrange("(kc p) n -> p kc n", p=128)
    for kc in range(8):
        P.dma("pool", wo[:, kc, :], wov[:, kc, :], w=["wo"], chan=f"pw{kc % 2}")
    xv = io["x"].rearrange("(n p) d -> n p d", p=128)
    x1v = io["x1"].rearrange("(n p) d -> n p d", p=128)
    for i in range(NT):
        xs = i % 2
        tsl = slice(128 * i, 128 * (i + 1))
        P.dma("sp", xt[xs], xv[i], w=[f"xt{xs}"], chan=f"px{xs}")
        for hf in range(2):
            pb = 2 * xs + hf
            for kc in range(8):
                P.op("pe", lambda e, hf=hf, kc=kc, tsl=tsl, pb=pb: e.matmul(
                    out=banks[pb], lhsT=mixT[:, kc, tsl], rhs=wo[:, kc, 512 * hf:512 * (hf + 1)],
                    start=(kc == 0), stop=(kc == 7)), r=["mixT", "wo"], w=[f"bank{pb}"])
            P.op("dve", lambda e, hf=hf, xs=xs, pb=pb: e.tensor_tensor(
                out=xs1[xs][:, 512 * hf:512 * (hf + 1)], in0=xt[xs][:, 512 * hf:512 * (hf + 1)], in1=banks[pb],
                op=ALU.add), r=[f"xt{xs}", f"bank{pb}"], w=[f"xs1{xs}"])
        P.dma("sp", x1v[i], xs1[xs], r=[f"xs1{xs}"], w=["o_x1"], chan=f"py{xs}")
    A.release()


def phase_peer(P, A, banks, C, T, io, last_layer):
    A.mark()
    NT = T // 128
    wqs = A.alloc([8, 2048], BF16)
    kks = A.alloc([16, 128], BF16)
    g2b = A.alloc([D], F32)
    gfb = A.alloc([D], F32) if last_layer else None
    x1 = [A.alloc([D], F32) for _ in range(3)]
    h2f = [A.alloc([D], F32) for _ in range(3)]
    eidx = [A.alloc([128], I32) for _ in range(2)]
    gate = [A.alloc([8, 16], F32) for _ in range(2)]
    sqj = A.alloc([D], F32)
    junkb = A.alloc([D], F32)
    ot = A.alloc([D], F32)
    ss = A.alloc([2], F32)
    h2b = A.alloc([D], BF16)
    h2T = A.alloc([8, 128], BF16)
    qTb = A.alloc([16, 128], BF16)
    scs2 = [A.alloc([16, 128], F32) for _ in range(2)]
    sc2 = A.alloc([128], F32)
    tv = A.alloc([16, 16], F32)
    ti = A.alloc([16, 16], U32)
    tif = A.alloc([16, 16], F32)
    cand = A.alloc([8, 256], F32)
    cand2 = A.alloc([256], F32)
    sv = A.alloc([8, 16], F32)
    ci = A.alloc([8, 16], U32)
    irow = A.alloc([8, 16], U32)
    jcol = A.alloc([8, 16], U32)
    irf = A.alloc([8, 16], F32)
    jcf = A.alloc([8, 16], F32)
    e1 = A.alloc([8, 16], F32)
    e2 = A.alloc([8, 16], F32)
    iot_i = A.alloc([16], I32)
    iot = A.alloc([16], F32)
    gsum = A.alloc([8], F32)
    hid = A.alloc([128], F32)
    wgt = A.alloc([128], F32)
    gb = [A.alloc([2 * D], BF16) for _ in range(NG)]
    dg = [A.alloc([128], BF16) for _ in range(ND)]
    gel = A.alloc([128], F32)

    wqv = io["wq"].rearrange("(kc p) n -> p kc n", p=128)
    for kc in range(8):
        P.dma("pool", wqs[:, kc, :], wqv[:, kc, :], w=["wqs"], chan=f"pq{kc % 2}")
    P.dma("pool", kks, io["kk"], w=["kks"], chan="pk")
    P.dma("sp", g2b, io["g2"].to_broadcast([128, D]), w=["g2b"], chan="p0")
    if last_layer:
        P.dma("sp", gfb, io["gf"].to_broadcast([128, D]), w=["gfb"], chan="p1")
    P.op("pool", lambda e: e.iota(out=iot_i, pattern=[[1, 16]], base=0, channel_multiplier=0), w=["iot_i"])
    P.op("dve", lambda e: e.tensor_copy(out=iot, in_=iot_i), r=["iot_i"], w=["iot"])

    xv = io["x1"].rearrange("(n p) d -> n p d", p=128)
    ov = io["xo"].rearrange("(n p) d -> n p d", p=128)
    tvv = tv.rearrange("p (h s) k -> p h s k", s=2)
    tifv = tif.rearrange("p (h s) k -> p h s k", s=2)
    cand4 = cand.rearrange("p h (i j) -> p h i j", j=16)
    msk = cand4
    iob = iot.unsqueeze(1).unsqueeze(1).to_broadcast([128, 8, 16, 16])
    tpv = banks[2].bitcast(BF16).rearrange("p (a b) -> p a b", b=128)

    def stage_a1(i):
        s3 = i % 3
        X1, H2F = x1[s3], h2f[s3]
        kx, kh = f"x1{s3}", f"h2f{s3}"
        scs = scs2[i % 2]
        ks = f"scs{i % 2}"
        P.dma("sp", X1, xv[i], w=[kx], chan=f"px{s3}")
        P.op("act", lambda e: e.activation(out=sqj, in_=X1, func=AF.Square, accum_out=ss[:, 0:1]),
             r=[kx], w=["sqj", "ss0"])
        P.op("act", lambda e: e.activation(out=ss[:, 0:1], in_=ss[:, 0:1], func=AF.Sqrt, scale=1.0 / D, bias=EPS),
             r=["ss0"], w=["ss0"])
        P.op("dve", lambda e: e.reciprocal(out=ss[:, 0:1], in_=ss[:, 0:1]), r=["ss0"], w=["ss0"])
        P.op("dve", lambda e: e.scalar_tensor_tensor(out=H2F, in0=X1, scalar=ss[:, 0:1], in1=g2b,
                                                      op0=ALU.mult, op1=ALU.mult), r=[kx, "ss0", "g2b"], w=[kh])
        P.op("act", lambda e: e.copy(out=h2b, in_=H2F), r=[kh], w=["h2b"])
        for kc in range(8):
            P.op("pe", lambda e, kc=kc: e.transpose(out=tpv[:, kc, :], in_=h2b[:, 128 * kc:128 * (kc + 1)],
                                                    identity=C.identb), r=["h2b", "identb"], w=["bank2"])
        P.op("act", lambda e: e.copy(out=h2T, in_=tpv), r=["bank2"], w=["h2T"])
        for qg in range(4):
            pb = 3 + (qg % 2)
            qps = banks[pb].rearrange("p (a b) -> p a b", b=128)
            for jj in range(4):
                j = 4 * qg + jj
                for kc in range(8):
                    P.op("pe", lambda e, qps=qps, jj=jj, j=j, kc=kc: e.matmul(
                        out=qps[:, jj, :], lhsT=wqs[:, kc, 128 * j:128 * (j + 1)], rhs=h2T[:, kc, :],
                        start=(kc == 0), stop=(kc == 7)), r=["wqs", "h2T"], w=[f"bank{pb}"])
            P.op("act", lambda e, qps=qps, qg=qg: e.copy(out=qTb[:, 4 * qg:4 * qg + 4, :], in_=qps),
                 r=[f"bank{pb}"], w=["qTb"])
        for qg in range(4):
            pb = 5 + (qg % 2)
            sps = banks[pb].rearrange("p (a b) -> p a b", b=128)
            for jj in range(4):
                j = 4 * qg + jj
                P.op("pe", lambda e, sps=sps, jj=jj, j=j: e.matmul(out=sps[:, jj, :], lhsT=qTb[:, j, :],
                                                                    rhs=kks[:, j, :], start=True, stop=True),
                     r=["qTb", "kks"], w=[f"bank{pb}"])
            P.op("act", lambda e, sps=sps, qg=qg: e.copy(out=scs[:, 4 * qg:4 * qg + 4, :], in_=sps),
                 r=[f"bank{pb}"], w=[ks])

    def stage_a2(i):
        sl = i % 2
        EIDX, GATE = eidx[sl], gate[sl]
        ke, kg = f"eidx{sl}", f"gate{sl}"
        scs = scs2[i % 2]
        ks = f"scs{i % 2}"
        for j in range(16):
            P.op("dve", lambda e, j=j: e.max(out=tv[:, j, 0:8], in_=scs[:, j, :]), r=[ks], w=["tv"])
            P.op("dve", lambda e, j=j: e.max_index(out=ti[:, j, 0:8], in_max=tv[:, j, 0:8], in_values=scs[:, j, :]),
                 r=[ks, "tv"], w=["ti"])
            P.op("dve", lambda e, j=j: e.match_replace(out=sc2, in_to_replace=tv[:, j, 0:8], in_values=scs[:, j, :],
                                                       imm_value=-1e30), r=[ks, "tv"], w=["sc2"])
            P.op("dve", lambda e, j=j: e.max(out=tv[:, j, 8:16], in_=sc2), r=["sc2"], w=["tv"])
            P.op("dve", lambda e, j=j: e.max_index(out=ti[:, j, 8:16], in_max=tv[:, j, 8:16], in_values=sc2),
                 r=["sc2", "tv"], w=["ti"])
        P.op("dve", lambda e: e.tensor_tensor(
            out=cand4, in0=tvv[:, :, 0, :].unsqueeze(3).to_broadcast([128, 8, 16, 16]),
            in1=tvv[:, :, 1, :].unsqueeze(2).to_broadcast([128, 8, 16, 16]), op=ALU.add), r=["tv"], w=["cand"])
        for h in range(8):
            P.op("dve", lambda e, h=h: e.max(out=sv[:, h, 0:8], in_=cand[:, h, :]), r=["cand"], w=["sv"])
            P.op("dve", lambda e, h=h: e.max_index(out=ci[:, h, 0:8], in_max=sv[:, h, 0:8], in_values=cand[:, h, :]),
                 r=["cand", "sv"], w=["ci"])
            P.op("dve", lambda e, h=h: e.match_replace(out=cand2, in_to_replace=sv[:, h, 0:8], in_values=cand[:, h, :],
                                                       imm_value=-1e30), r=["cand", "sv"], w=["cand2"])
            P.op("dve", lambda e, h=h: e.max(out=sv[:, h, 8:16], in_=cand2), r=["cand2"], w=["sv"])
            P.op("dve", lambda e, h=h: e.max_index(out=ci[:, h, 8:16], in_max=sv[:, h, 8:16], in_values=cand2),
                 r=["cand2", "sv"], w=["ci"])
        P.op("dve", lambda e: e.tensor_single_scalar(out=irow, in_=ci, scalar=4, op=ALU.logical_shift_right),
             r=["ci"], w=["irow"])
        P.op("dve", lambda e: e.tensor_single_scalar(out=jcol, in_=ci, scalar=15, op=ALU.bitwise_and),
             r=["ci"], w=["jcol"])
        P.op("dve", lambda e: e.tensor_copy(out=irf, in_=irow), r=["irow"], w=["irf"])
        P.op("dve", lambda e: e.tensor_copy(out=jcf, in_=jcol), r=["jcol"], w=["jcf"])
        P.op("dve", lambda e: e.tensor_copy(out=tif, in_=ti), r=["ti"], w=["tif"])
        for (src, half, dst, dk) in ((irf, 0, e1, "e1"), (jcf, 1, e2, "e2")):
            P.op("dve", lambda e, src=src: e.tensor_tensor(
                out=msk, in0=src.unsqueeze(3).to_broadcast([128, 8, 16, 16]), in1=iob, op=ALU.is_equal),
                r=["irf", "jcf", "iot"], w=["cand"])
            P.op("dve", lambda e, half=half: e.tensor_tensor(
                out=msk, in0=msk, in1=tifv[:, :, half, :].unsqueeze(2).to_broadcast([128, 8, 16, 16]),
                op=ALU.mult), r=["cand", "tif"], w=["cand"])
            P.op("dve", lambda e, dst=dst: e.tensor_reduce(out=dst, in_=msk, axis=AX.X, op=ALU.add),
                 r=["cand"], w=[dk])
        P.op("dve", lambda e: e.scalar_tensor_tensor(out=e1, in0=e1, scalar=128.0, in1=e2, op0=ALU.mult, op1=ALU.add),
             r=["e1", "e2"], w=["e1"])
        P.op("dve", lambda e: e.tensor_copy(out=EIDX, in_=e1.rearrange("p h k -> p (h k)")), r=["e1"], w=[ke])
        P.op("dve", lambda e: e.tensor_tensor(out=GATE, in0=sv, in1=sv[:, :, 0:1].to_broadcast([128, 8, 16]),
                                              op=ALU.subtract), r=["sv"], w=[kg])
        P.op("act", lambda e: e.activation(out=GATE, in_=GATE, func=AF.Exp), r=[kg], w=[kg])
        P.op("dve", lambda e: e.tensor_reduce(out=gsum, in_=GATE, axis=AX.X, op=ALU.add), r=[kg], w=["gsum"])
        P.op("dve", lambda e: e.reciprocal(out=gsum, in_=gsum), r=["gsum"], w=["gsum"])
        P.op("dve", lambda e: e.tensor_tensor(out=GATE, in0=GATE, in1=gsum.unsqueeze(2).to_broadcast([128, 8, 16]),
                                              op=ALU.mult), r=[kg, "gsum"], w=[kg])

    def stage_b(i):
        sl = i % 2
        s3 = i % 3
        X1, H2F, EIDX, GATE = x1[s3], h2f[s3], eidx[sl], gate[sl]
        kx, kh, ke, kg = f"x1{s3}", f"h2f{s3}", f"eidx{sl}", f"gate{sl}"
        uvk = io["uvb_keys"]
        for s in range(128):
            k = s % NG
            kd = s % ND
            P.op("pool", lambda e, s=s, k=k: e.indirect_dma_start(
                out=gb[k], out_offset=None, in_=io["uvb"],
                in_offset=bass.IndirectOffsetOnAxis(ap=EIDX[:, s:s + 1], axis=0)),
                r=[ke] + uvk, w=[f"gb{k}"], chan=f"gg{k}")
            P.op("dve", lambda e, s=s, k=k: e.scalar_tensor_tensor(
                out=junkb, in0=gb[k][:, 0:D], scalar=1.0, in1=H2F, op0=ALU.mult, op1=ALU.mult,
                accum_out=hid[:, s:s + 1]), r=[f"gb{k}", kh], w=["junkb", f"hid{s % 8}"])
            P.op("act", lambda e, s=s: e.activation(out=gel[:, s:s + 1], in_=hid[:, s:s + 1], func=AF.Gelu_apprx_tanh),
                 r=[f"hid{s % 8}"], w=[f"gel{s % 8}"])
            P.op("act", lambda e, s=s: e.activation(out=wgt[:, s:s + 1], in_=gel[:, s:s + 1], func=AF.Copy,
                                                    scale=GATE.rearrange("p h k -> p (h k)")[:, s:s + 1]),
                 r=[f"gel{s % 8}", kg], w=[f"wgt{s % 8}"])
            P.op("act", lambda e, s=s, kd=kd: e.activation(out=dg[kd], in_=C.identf, func=AF.Copy,
                                                           scale=wgt[:, s:s + 1]),
                 r=["identf", f"wgt{s % 8}"], w=[f"dg{kd}"])
            for hf in range(2):
                P.op("pe", lambda e, s=s, k=k, kd=kd, hf=hf: e.matmul(
                    out=banks[hf], lhsT=dg[kd], rhs=gb[k][:, D + 512 * hf:D + 512 * (hf + 1)],
                    start=(s == 0), stop=(s == 127)), r=[f"dg{kd}", f"gb{k}"], w=[f"bank{hf}"])
        for hf in range(2):
            P.op("dve", lambda e, hf=hf: e.tensor_tensor(out=ot[:, 512 * hf:512 * (hf + 1)],
                                                         in0=X1[:, 512 * hf:512 * (hf + 1)], in1=banks[hf], op=ALU.add),
                 r=[kx, f"bank{hf}"], w=["ot"])
        if last_layer:
            P.op("act", lambda e: e.activation(out=junkb, in_=ot, func=AF.Square, accum_out=ss[:, 1:2]),
                 r=["ot"], w=["junkb", "ss1"])
            P.op("act", lambda e: e.activation(out=ss[:, 1:2], in_=ss[:, 1:2], func=AF.Sqrt, scale=1.0 / D, bias=EPS),
                 r=["ss1"], w=["ss1"])
            P.op("dve", lambda e: e.reciprocal(out=ss[:, 1:2], in_=ss[:, 1:2]), r=["ss1"], w=["ss1"])
            P.op("dve", lambda e: e.scalar_tensor_tensor(out=ot, in0=ot, scalar=ss[:, 1:2], in1=gfb,
                                                          op0=ALU.mult, op1=ALU.mult), r=["ot", "ss1", "gfb"], w=["ot"])
        P.dma("sp", ov[i], ot, r=["ot"], w=["o_x"], chan="po", final=True)

    stage_a1(0)
    if NT > 1:
        stage_a1(1)
    stage_a2(0)
    for i in range(NT):
        if i + 2 < NT:
            stage_a1(i + 2)
        if i + 1 < NT:
            stage_a2(i + 1)
        stage_b(i)
    A.release()


LAYER_W = {
    "w_in": ([D, N_IN], F32), "g1": ([1, D], F32), "bfg": ([8, 1], F32),
    "cw": ([256, 31], F32), "cpar": ([256, 3], F32), "rw": ([256, 4], F32), "rpar": ([256, 4], F32),
    "wr": ([4, 64, 64], F32), "wi": ([4, 64, 64], F32), "w_out": ([D, D], F32), "g2": ([1, D], F32),
    "wq": ([D, 2048], F32), "kk": ([128, 16, 128], F32), "puv": ([16384, 2 * D], F32),
}


def build_fused(T):
    nc = bass.Bass("TRN2", target_bir_lowering=False)
    P = Prog(nc)
    xin = _dram(nc, "x2", [2 * T, D], F32, "ExternalInput")
    flg = _dram(nc, "flags", [128, 2], F32, "ExternalInput")
    gf = _dram(nc, "gf", [1, D], F32, "ExternalInput")
    W = [{k: _dram(nc, f"{k}_l{l}", shp, dt, "ExternalInput") for k, (shp, dt) in LAYER_W.items()} for l in range(2)]
    out = _dram(nc, "out", [T, D], F32, "ExternalOutput")
    scr = [{k: _dram(nc, f"s{sl}_{k}", shp, dt, "Internal") for k, (shp, dt) in P1_OUT(T).items()} for sl in range(2)]
    xmid = [_dram(nc, f"xmid{sl}", [T, D], F32, "Internal") for sl in range(2)]
    x1s = _dram(nc, "x1s", [T, D], F32, "Internal")
    uvb = [_dram(nc, f"uvb{l}", [16384, 2 * D], BF16, "Internal") for l in range(2)]
    NCV = 8
    RCV = 16384 // NCV

    def convert(l):
        for j in range(NCV):
            P.dma("pool", uvb[l][RCV * j:RCV * (j + 1), :], W[l]["puv"][RCV * j:RCV * (j + 1), :],
                  w=[f"uvb{l}_{j}"], chan=f"cv{j % 4}")
    A = Arena(P, 206 * 1024)
    banks = [P.ps(f"bank{i}", [128, 512], F32)[:, :] for i in range(8)]
    C = setup_common(P, A, banks)
    flags1 = A.alloc([2], F32)
    flags0 = A.alloc([2], F32)
    P.dma("sp", flags1, flg, w=["flags"], chan="f0")
    P.op("pool", lambda e: e.memset(flags0[:, 0:1], MASKV), w=["flags"])
    P.op("pool", lambda e: e.memset(flags0[:, 1:2], 0.0), w=["flags"])

    def p1(l, sl, xsrc):
        io = dict(scr[sl])
        io.update(x=xsrc, w_in=W[l]["w_in"], g1=W[l]["g1"], bfg=W[l]["bfg"])
        P.barrier()
        phase1(P, A, banks, C, T, io)

    def p2(l, sl, xsrc, xdst, last):
        prev = scr[0]
        own = scr[sl]
        io = dict(W[l])
        io.update(qT=own["qT"], cq3=own["cq3"], ggT=own["ggT"], kT_prev=prev["kT"], kT_own=own["kT"],
                  v_prev=prev["v"], v_own=own["v"], kb_prev=prev["kb_next"], kb_own=own["kb_own"],
                  y_prev=prev["yT"], y_own=own["yT"], xr_prev=prev["xrT"], xr_own=own["xrT"],
                  x=xsrc, xo=xdst, gf=gf, x1=x1s, uvb=uvb[l], uvb_keys=[f"uvb{l}_{j}" for j in range(NCV)])
        fl = flags0 if sl == 0 else flags1
        P.barrier()
        A.mark()
        mixT = A.alloc([8, T], BF16)
        phase_rnn(P, A, banks, C, T, io, mixT, fl, has_prev=(sl == 1))
        P.barrier()
        phase_conv(P, A, banks, C, T, io, mixT, fl)
        P.barrier()
        phase_attn(P, A, banks, C, T, io, mixT, fl, has_prev=(sl == 1))
        P.barrier()
        phase_wout(P, A, banks, C, T, io, mixT)
        A.release()
        P.barrier()
        phase_peer(P, A, banks, C, T, io, last)

    x0, x1 = xin[0:T, :], xin[T:2 * T, :]
    p1(0, 0, x0)
    convert(0)
    p1(0, 1, x1)
    convert(1)
    p2(0, 0, x0, xmid[0], False)
    p2(0, 1, x1, xmid[1], False)
    p1(1, 0, xmid[0])
    p1(1, 1, xmid[1])
    p2(1, 1, xmid[1], out, True)
    P.emit()
    return nc, P


def layer_weights(inp, l):
    k1 = inp["peer_k1"][l]
    k2 = inp["peer_k2"][l]
    kk = np.empty((128, 16, 128), np.float32)
    for h in range(8):
        kk[:, 2 * h, :] = k1[h].T
        kk[:, 2 * h + 1, :] = k2[h].T
    c = np.ascontiguousarray
    return {
        "w_in": inp["w_in"][l], "g1": inp["norm1_g"][l][None, :], "bfg": c(inp["b_forget"][l][:, None]),
        "cw": c(inp["conv_dw_w"][l].T),
        "cpar": c(np.stack([inp["conv_dw_b"][l], inp["conv_ln_g"][l], inp["conv_ln_b"][l]], 1)),
        "rw": c(inp["rg_conv_w"][l].T),
        "rpar": c(np.stack([inp["rg_conv_b"][l], inp["rg_b_r"][l], inp["rg_b_i"][l], inp["rg_lambda"][l]], 1)),
        "wr": inp["rg_w_r"][l], "wi": inp["rg_w_i"][l], "w_out": inp["w_out"][l], "g2": inp["norm2_g"][l][None, :],
        "wq": inp["peer_wq"][l], "kk": kk, "puv": np.concatenate([inp["peer_u"][l], inp["peer_v"][l]], axis=1),
    }


def make_in_maps(inp, T, ncore, seq_of_core):
    shared = {"gf": np.ascontiguousarray(inp["final_g"][None, :])}
    for l in range(2):
        for k, v in layer_weights(inp, l).items():
            shared[f"{k}_l{l}"] = np.ascontiguousarray(v.astype(np.float32))
    maps = []
    for c in range(ncore):
        b, half = seq_of_core(c)
        m = dict(shared)
        fl = np.zeros((128, 2), np.float32)
        if half == 0:
            fl[:, 0] = MASKV
            fl[:, 1] = 0.0
            m["x2"] = np.ascontiguousarray(np.concatenate([inp["x"][b, 0:T], inp["x"][b, 0:T]], 0))
        else:
            fl[:, 1] = 1.0
            m["x2"] = np.ascontiguousarray(inp["x"][b, 0:2 * T])
        m["flags"] = fl
        maps.append(m)
    return maps


T_CORE = 4096


def kernel(**inputs):
    inp = {k: np.asarray(v) for k, v in inputs.items()}
    T = T_CORE
    cores = list(range(8))
    nc, _ = build_fused(T)
    maps = make_in_maps(inp, T, 8, lambda c: (c // 2, c % 2))
    res = run_bass_kernel_spmd(nc, maps, core_ids=cores).results
    xs = [np.asarray(res[c]["out"]) for c in cores]
    out = np.stack([np.concatenate([xs[2 * b], xs[2 * b + 1]], 0) for b in range(4)], 0)
    return out.astype(np.float32)
```
